# Optimizing a Trainium2 kernel written in Bass

```python
import math
import jax, jax.numpy as jnp
from jax import lax
import numpy as np

D_MODEL = 2048
BATCH = 8
SEQ = 4096
DEPTH = 4

SSM_WIDTH = D_MODEL // 2
SSM_GROUP = 16
SSM_GROUPS = SSM_WIDTH // SSM_GROUP
SSM_STATE = 64
ATTN_WIDTH = D_MODEL - SSM_WIDTH
HEAD_DIM = 64
ATTN_HEADS = ATTN_WIDTH // (2 * HEAD_DIM)
MIX_WIDTH = SSM_WIDTH + ATTN_WIDTH
IN_WIDTH = 2 * SSM_WIDTH + 4 * ATTN_WIDTH
ROPE_THETA = 10000.0
Q_BLOCK = 128
NORM_EPS = 1e-6
DT_MIN = 1e-3
DT_MAX = 1e-1

kernel_name = "hybrid_s5_diffattn_adaln_trunk"


def rms_norm(x, g):
    xf = x.astype(jnp.float32)
    y = xf * lax.rsqrt(jnp.mean(xf * xf, axis=-1, keepdims=True) + NORM_EPS)
    return (y * g.astype(jnp.float32)).astype(x.dtype)


def rope_tables(positions):
    half = HEAD_DIM // 2
    inv_freq = ROPE_THETA ** (-jnp.arange(half, dtype=jnp.float32) / half)
    ang = positions.astype(jnp.float32)[..., None] * inv_freq
    return jnp.cos(ang)[:, :, None, None, :], jnp.sin(ang)[:, :, None, None, :]


def apply_rope(t, cos, sin):
    half = HEAD_DIM // 2
    tf = t.astype(jnp.float32)
    t1, t2 = tf[..., :half], tf[..., half:]
    return jnp.concatenate([t1 * cos - t2 * sin, t2 * cos + t1 * sin], axis=-1).astype(t.dtype)


def s5_mixer(u, a_re, a_im, b_re, b_im, c_re, c_im, d_skip, log_step, w_glu, b_glu):
    bsz, s = u.shape[0], u.shape[1]
    uf = u.astype(jnp.float32).reshape(bsz, s, SSM_GROUPS, SSM_GROUP)
    lam = lax.complex(a_re.astype(jnp.float32), a_im.astype(jnp.float32))
    step = jnp.exp(log_step.astype(jnp.float32))[:, None]
    a_bar = jnp.exp(lam * step)
    b_mat = lax.complex(b_re.astype(jnp.float32), b_im.astype(jnp.float32))
    b_bar = ((a_bar - 1.0) / lam)[..., None] * b_mat
    bu = jnp.einsum('gnp,bsgp->bsgn', b_bar, uf)
    a_seq = jnp.broadcast_to(a_bar, (1, s, SSM_GROUPS, SSM_STATE))

    def combine(left, right):
        a_l, b_l = left
        a_r, b_r = right
        return a_r * a_l, a_r * b_l + b_r

    _, states = lax.associative_scan(combine, (a_seq, bu), axis=1)
    c_mat = lax.complex(c_re.astype(jnp.float32), c_im.astype(jnp.float32))
    y = jnp.einsum('gpn,bsgn->bsgp', c_mat, states).real + d_skip.astype(jnp.float32) * uf
    y = jax.nn.gelu(y.reshape(bsz, s, SSM_WIDTH))
    ab = y @ w_glu.astype(jnp.float32) + b_glu.astype(jnp.float32)
    out = ab[..., :SSM_WIDTH] * jax.nn.sigmoid(ab[..., SSM_WIDTH:])
    return out.astype(u.dtype)


def diff_attention(q, k, v, lam, sub_g, lambda_init):
    bsz, s = q.shape[0], q.shape[1]
    nblk = s // Q_BLOCK
    scale = HEAD_DIM ** -0.5
    qb = q.reshape(bsz, nblk, Q_BLOCK, ATTN_HEADS, 2, HEAD_DIM).transpose(1, 0, 2, 3, 4, 5)
    vf = v.astype(jnp.float32)
    key_pos = jnp.arange(s)

    def block(args):
        q_blk, i = args
        sc = jnp.einsum('bqhmd,bkhmd->bhmqk', q_blk, k,
                        preferred_element_type=jnp.float32) * scale
        q_pos = i * Q_BLOCK + jnp.arange(Q_BLOCK)
        mask = key_pos[None, :] <= q_pos[:, None]
        sc = jnp.where(mask, sc, -jnp.inf)
        p = jax.nn.softmax(sc, axis=-1)
        w = p[:, :, 0] - lam * p[:, :, 1]
        return jnp.einsum('bhqk,bkhe->bqhe', w, vf)

    out = lax.map(block, (qb, jnp.arange(nblk)))
    out = out.transpose(1, 0, 2, 3, 4).reshape(bsz, s, ATTN_HEADS, 2 * HEAD_DIM)
    out = rms_norm(out, sub_g) * (1.0 - lambda_init)
    return out.astype(v.dtype)


def setup_inputs(seed: int = 0) -> dict:
    key = jax.random.key(seed)
    ks = jax.random.split(key, 24)
    f32 = jnp.float32
    x = jax.random.normal(ks[0], (BATCH, SEQ, D_MODEL), f32)
    c = jax.random.normal(ks[1], (BATCH, D_MODEL), f32)
    offset = jax.random.randint(ks[2], (BATCH, 1), 0, 2048, dtype=jnp.int32)
    positions = (offset + jnp.arange(SEQ, dtype=jnp.int32)[None, :]).astype(jnp.int32)
    norm_g = 1.0 + 0.02 * jax.random.normal(ks[3], (DEPTH, D_MODEL), f32)
    w_ada = 0.5 * D_MODEL ** -0.5 * jax.random.normal(ks[4], (DEPTH, D_MODEL, 3 * D_MODEL), f32)
    b_ada = 0.02 * jax.random.normal(ks[5], (DEPTH, 3 * D_MODEL), f32)
    w_in = D_MODEL ** -0.5 * jax.random.normal(ks[6], (DEPTH, D_MODEL, IN_WIDTH), f32)
    w_out = MIX_WIDTH ** -0.5 * jax.random.normal(ks[7], (DEPTH, MIX_WIDTH, D_MODEL), f32)
    ssm_a_re = -0.5 + 0.01 * jax.random.normal(ks[8], (DEPTH, SSM_GROUPS, SSM_STATE), f32)
    ssm_a_im = (math.pi * jnp.arange(SSM_STATE, dtype=f32))[None, None, :] \
        + 0.01 * jax.random.normal(ks[9], (DEPTH, SSM_GROUPS, SSM_STATE), f32)
    ssm_b_re = (2 * SSM_GROUP) ** -0.5 * jax.random.normal(ks[10], (DEPTH, SSM_GROUPS, SSM_STATE, SSM_GROUP), f32)
    ssm_b_im = (2 * SSM_GROUP) ** -0.5 * jax.random.normal(ks[11], (DEPTH, SSM_GROUPS, SSM_STATE, SSM_GROUP), f32)
    ssm_c_re = (2 * SSM_STATE) ** -0.5 * jax.random.normal(ks[12], (DEPTH, SSM_GROUPS, SSM_GROUP, SSM_STATE), f32)
    ssm_c_im = (2 * SSM_STATE) ** -0.5 * jax.random.normal(ks[13], (DEPTH, SSM_GROUPS, SSM_GROUP, SSM_STATE), f32)
    ssm_d = jax.random.normal(ks[14], (DEPTH, SSM_GROUPS, SSM_GROUP), f32)
    ssm_log_step = jax.random.uniform(ks[15], (DEPTH, SSM_GROUPS), f32,
                                      minval=math.log(DT_MIN), maxval=math.log(DT_MAX))
    w_glu = SSM_WIDTH ** -0.5 * jax.random.normal(ks[16], (DEPTH, SSM_WIDTH, 2 * SSM_WIDTH), f32)
    b_glu = 0.02 * jax.random.normal(ks[17], (DEPTH, 2 * SSM_WIDTH), f32)
    lam_q1 = 0.1 * jax.random.normal(ks[18], (DEPTH, HEAD_DIM), f32)
    lam_k1 = 0.1 * jax.random.normal(ks[19], (DEPTH, HEAD_DIM), f32)
    lam_q2 = 0.1 * jax.random.normal(ks[20], (DEPTH, HEAD_DIM), f32)
    lam_k2 = 0.1 * jax.random.normal(ks[21], (DEPTH, HEAD_DIM), f32)
    sub_g = 1.0 + 0.02 * jax.random.normal(ks[22], (DEPTH, 2 * HEAD_DIM), f32)
    final_g = 1.0 + 0.02 * jax.random.normal(ks[23], (D_MODEL,), f32)
    return {"x": x, "c": c, "positions": positions, "norm_g": norm_g, "w_ada": w_ada, "b_ada": b_ada,
            "w_in": w_in, "w_out": w_out, "ssm_a_re": ssm_a_re, "ssm_a_im": ssm_a_im,
            "ssm_b_re": ssm_b_re, "ssm_b_im": ssm_b_im, "ssm_c_re": ssm_c_re, "ssm_c_im": ssm_c_im,
            "ssm_d": ssm_d, "ssm_log_step": ssm_log_step, "w_glu": w_glu, "b_glu": b_glu,
            "lam_q1": lam_q1, "lam_k1": lam_k1, "lam_q2": lam_q2, "lam_k2": lam_k2,
            "sub_g": sub_g, "final_g": final_g}


def reference(x, c, positions, norm_g, w_ada, b_ada, w_in, w_out, ssm_a_re, ssm_a_im,
              ssm_b_re, ssm_b_im, ssm_c_re, ssm_c_im, ssm_d, ssm_log_step, w_glu, b_glu,
              lam_q1, lam_k1, lam_q2, lam_k2, sub_g, final_g):
    bsz, s = x.shape[0], x.shape[1]
    cos, sin = rope_tables(positions)
    c_act = jax.nn.silu(c)
    splits = [SSM_WIDTH, 2 * SSM_WIDTH, 2 * SSM_WIDTH + ATTN_WIDTH,
              2 * SSM_WIDTH + 2 * ATTN_WIDTH, 2 * SSM_WIDTH + 3 * ATTN_WIDTH]
    for l in range(DEPTH):
        lambda_init = 0.8 - 0.6 * math.exp(-0.3 * l)
        mod = c_act @ w_ada[l] + b_ada[l]
        shift, scale, gate = jnp.split(mod, 3, axis=-1)
        h = rms_norm(x, norm_g[l]) * (1.0 + scale[:, None, :]) + shift[:, None, :]
        proj = h @ w_in[l]
        u, z_s, q, k, v, z_a = jnp.split(proj, splits, axis=-1)
        y_s = s5_mixer(u, ssm_a_re[l], ssm_a_im[l], ssm_b_re[l], ssm_b_im[l], ssm_c_re[l],
                       ssm_c_im[l], ssm_d[l], ssm_log_step[l], w_glu[l], b_glu[l]) * jax.nn.silu(z_s)
        q = apply_rope(q.reshape(bsz, s, ATTN_HEADS, 2, HEAD_DIM), cos, sin)
        k = apply_rope(k.reshape(bsz, s, ATTN_HEADS, 2, HEAD_DIM), cos, sin)
        v = v.reshape(bsz, s, ATTN_HEADS, 2 * HEAD_DIM)
        lam = (jnp.exp(jnp.sum(lam_q1[l].astype(jnp.float32) * lam_k1[l].astype(jnp.float32)))
               - jnp.exp(jnp.sum(lam_q2[l].astype(jnp.float32) * lam_k2[l].astype(jnp.float32)))
               + lambda_init)
        y_a = diff_attention(q, k, v, lam, sub_g[l], lambda_init).reshape(bsz, s, ATTN_WIDTH)
        y_a = y_a * jax.nn.silu(z_a)
        y = jnp.concatenate([y_s, y_a.astype(y_s.dtype)], axis=-1) @ w_out[l]
        x = x + gate[:, None, :] * y
    return rms_norm(x, final_g)
```

```python
import math
import numpy as np
from contextlib import ExitStack
import concourse.bass as bass
import concourse.mybir as mybir
from concourse.bass_utils import run_bass_kernel_spmd

F32 = mybir.dt.float32
BF16 = mybir.dt.bfloat16
I32 = mybir.dt.int32
AF = mybir.ActivationFunctionType
ALU = mybir.AluOpType
AX = mybir.AxisListType

S = 4096
D = 2048
DEPTH = 4
NT = 8
TW = 512
LCH = 8
NCH = S // LCH
NSCAN = 9
KEXP = list(range(0, LCH + 1)) + [LCH * (2 ** s) for s in range(1, NSCAN)]
NSEM = 50
TWO_PI = 2.0 * math.pi
import os
CASTENG = os.environ.get('CASTENG', 'gpsimd')
STENG = os.environ.get('STENG', 'sync')


class Prog:
    ENG = ("tensor", "vector", "scalar", "gpsimd", "sync")

    def __init__(self, nc, sems, clear):
        self.nc = nc
        self.sems = sems
        self.clear = clear
        self.ops = []
        self.lastw = {}
        self.readers = {}
        self.chan_last = {}
        self.chan_cnt = {}
        self.chan_eng = {}
        self.rings = {}
        self.fence_deps = set()
        self.fence_pending = set()

    def fence(self):
        last = {}
        for i, o in enumerate(self.ops):
            last[("c", o["chan"]) if o["chan"] is not None else ("e", o["eng"])] = i
        self.fence_deps = set(last.values())
        self.fence_pending = set(self.ENG)

    def ring(self, name, n):
        i = self.rings.get(name, 0)
        self.rings[name] = i + 1
        return i % n

    def op(self, eng, fn, reads=(), writes=(), chan=None, ndma=0):
        i = len(self.ops)
        deps = set()
        if eng in self.fence_pending:
            self.fence_pending.discard(eng)
            deps |= self.fence_deps
        for k in reads:
            if k in self.lastw:
                deps.add(self.lastw[k])
        for k in writes:
            if k in self.lastw:
                deps.add(self.lastw[k])
            deps.update(self.readers.get(k, ()))
        if chan is not None:
            if chan in self.chan_last:
                deps.add(self.chan_last[chan])
            self.chan_last[chan] = i
            self.chan_cnt[chan] = self.chan_cnt.get(chan, 0) + 16 * ndma
            self.chan_eng[chan] = eng
        self.ops.append(dict(eng=eng, fn=fn, deps=deps, chan=chan, ndma=ndma,
                             chan_val=self.chan_cnt.get(chan, 0) if chan is not None else None, sig=False))
        for k in reads:
            self.readers.setdefault(k, []).append(i)
        for k in writes:
            self.lastw[k] = i
            self.readers[k] = []
        return i

    def act(self, out, in_, func, reads, writes, **kw):
        self.op("scalar", lambda e: e.activation(out=out, in_=in_, func=func, **kw), reads, writes)

    def tt(self, eng, out, in0, in1, op, reads, writes):
        self.op(eng, lambda e: e.tensor_tensor(out=out, in0=in0, in1=in1, op=op), reads, writes)

    def ts(self, eng, out, in0, s1, s2, op0, op1, reads, writes):
        if s2 is None:
            self.op(eng, lambda e: e.tensor_scalar(out=out, in0=in0, scalar1=s1, scalar2=None, op0=op0), reads, writes)
        else:
            self.op(eng, lambda e: e.tensor_scalar(out=out, in0=in0, scalar1=s1, scalar2=s2, op0=op0, op1=op1), reads, writes)

    def stt(self, out, in0, scalar, in1, op0, op1, reads, writes):
        self.op("vector", lambda e: e.scalar_tensor_tensor(out=out, in0=in0, scalar=scalar, in1=in1, op0=op0, op1=op1),
                reads, writes)

    def copy(self, eng, out, in_, reads, writes):
        if eng == "scalar":
            self.op(eng, lambda e: e.activation(out=out, in_=in_, func=AF.Copy), reads, writes)
        else:
            self.op(eng, lambda e: e.tensor_copy(out=out, in_=in_), reads, writes)

    def memset(self, eng, ap, val, writes):
        self.op(eng, lambda e: e.memset(ap, val), (), writes)

    def mm(self, items, reads, writes):
        def f(e):
            r = None
            for it in items:
                (o, l, rh, st, sp) = it[:5]
                if len(it) > 5 and it[5] is not None:
                    r = e.matmul(o, lhsT=l, rhs=rh, start=st, stop=sp, tile_position=it[5])
                else:
                    r = e.matmul(o, lhsT=l, rhs=rh, start=st, stop=sp)
            return r
        self.op("tensor", f, reads, writes)

    def tr(self, items, ident, reads, writes):
        def f(e):
            r = None
            for it in items:
                (o, i) = it[:2]
                if len(it) > 2 and it[2] is not None:
                    r = e.transpose(out=o, in_=i, identity=ident, tile_position=it[2])
                else:
                    r = e.transpose(out=o, in_=i, identity=ident)
            return r
        self.op("tensor", f, reads, writes)

    def dma(self, eng, pairs, reads, writes, chan):
        self.op(eng, lambda e: [e.dma_start(out=o, in_=i) for (o, i) in pairs], reads, writes, chan=chan, ndma=len(pairs))

    def finalize(self):
        nc = self.nc
        ops = self.ops
        for o in ops:
            for d in o["deps"]:
                ops[d]["sig"] = True
        cnt = {e: 0 for e in self.ENG}
        for o in ops:
            if o["chan"] is None and o["sig"]:
                cnt[o["eng"]] += 1
                o["sigval"] = cnt[o["eng"]]
        assert len(self.ENG) + len(self.chan_cnt) <= len(self.sems), (len(self.chan_cnt), "too many dma channels")
        esem = {e: self.sems[j] for j, e in enumerate(self.ENG)}
        csem = {c: self.sems[len(self.ENG) + j] for j, c in enumerate(self.chan_cnt)}
        per = {e: [] for e in self.ENG}
        for o in ops:
            per[o["eng"]].append(o)

        def run(engname):
            def body(eng):
                if engname == "sync":
                    for s in self.clear:
                        eng.sem_clear(s)
                waited = {}
                for o in per[engname]:
                    need = {}
                    for d in o["deps"]:
                        p = ops[d]
                        if p["chan"] is not None:
                            s, v = csem[p["chan"]], p["chan_val"]
                        else:
                            if p["eng"] == engname and engname == "tensor":
                                continue
                            s, v = esem[p["eng"]], p["sigval"]
                        if need.get(s, (None, 0))[1] < v:
                            need[s] = (s, v)
                    for s, v in need.values():
                        if waited.get(s, 0) < v:
                            eng.wait_ge(s, v)
                            waited[s] = v
                    r = o["fn"](eng)
                    if o["chan"] is not None:
                        assert len(r) == o["ndma"]
                        for ins in r:
                            ins.then_inc(csem[o["chan"]], 16)
                    elif o["sig"]:
                        r.then_inc(esem[engname], 1)
                for c, e_ in self.chan_eng.items():
                    if e_ == engname:
                        eng.wait_ge(csem[c], self.chan_cnt[c])
            return body

        with nc.Block() as block:
            block.tensor(run("tensor"))
            block.vector(run("vector"))
            block.scalar(run("scalar"))
            block.gpsimd(run("gpsimd"))
            block.sync(run("sync"))


class Builder:
    def __init__(self, depth=DEPTH, dbg=None, stop=99):
        self.stop = stop
        self.depth = depth
        self.dbg = dbg or []
        self.nc = bass.Bass("TRN2", target_bir_lowering=False)
        self.phase_idx = 0

    def din(self, name, shape, dt=F32):
        return self.nc.dram_tensor(name, list(shape), dt, kind="ExternalInput").ap()

    def dscr(self, name, shape, dt):
        return self.nc.dram_tensor(name, list(shape), dt, kind="Internal").ap()

    def tn(self, n):
        return "%s_p%d" % (n, self.phase_idx)

    def PS(self, st, name, shape, dt):
        return st.enter_context(self.nc.psum_tensor(self.tn(name), shape, dt))

    def new_prog(self, dummy=True):
        if dummy:
            self.new_prog(dummy=False).finalize()
        k = self.phase_idx
        self.phase_idx += 1
        return Prog(self.nc, self.semsets[k % 2], self.semsets[(k + 1) % 2])

    def build(self):
        nc = self.nc
        L = DEPTH
        d = {}
        d["xT"] = self.din("xT", [D, S])
        d["c_l"] = self.din("c_l", [128, 16])
        d["pos"] = self.din("pos", [1, S], I32)
        d["invf"] = self.din("invf", [128, 1])
        d["kvec"] = self.din("kvec", [128, len(KEXP)])
        d["normg"] = self.din("normg", [128, L, 16])
        d["w_ada"] = self.din("w_ada", [L, D, 3 * D])
        d["bada"] = self.din("bada", [128, L, 48])
        d["w_in"] = self.din("w_in", [L, D, 6144])
        d["w_out"] = self.din("w_out", [L, D, D])
        d["w_glu"] = self.din("w_glu", [L, 1024, 2048])
        d["bglu"] = self.din("bglu", [128, L, 16])
        for nm in ("are", "aim", "lst"):
            d[nm] = self.din(nm, [L, 128, 32])
        for nm in ("Bre", "Bim", "Cre", "Cim"):
            d[nm] = self.din(nm, [L, 128, 32, 16])
        d["ssmD"] = self.din("ssmD", [128, L, 8])
        for nm in ("lq1", "lk1", "lq2", "lk2"):
            d[nm] = self.din(nm, [128, L, 64])
        d["subg"] = self.din("subg", [128, L])
        d["fing"] = self.din("fing", [128, 16])
        d["outT"] = nc.dram_tensor("outT", [D, S], F32, kind="ExternalOutput").ap()
        d["xres"] = self.dscr("xres", [D, S], F32)
        d["hT"] = self.dscr("hT", [D, S], BF16)
        d["uT"] = self.dscr("uT", [1024, S], BF16)
        d["zsT"] = self.dscr("zsT", [1024, S], BF16)
        d["qS"] = self.dscr("qS", [1024, S], BF16)
        d["kS"] = self.dscr("kS", [1024, S], BF16)
        d["vS"] = self.dscr("vS", [S, 1024], BF16)
        d["zaT"] = self.dscr("zaT", [1024, S], BF16)
        d["ygT"] = self.dscr("ygT", [1024, S], BF16)
        d["yT"] = self.dscr("yT", [D, S], BF16)
        d["Sloc"] = self.dscr("Sloc", [32 * 128, 2 * NCH], F32)
        d["Xps"] = self.dscr("Xps", [32 * 128, 2 * NCH], BF16)
        d["WCs"] = self.dscr("WCs", [2 * 128, 32 * LCH * 32], BF16)
        d["KTs"] = self.dscr("KTs", [128, 8 * LCH * 128], BF16)
        d["PWs"] = self.dscr("PWs", [3 * 128, 32 * len(KEXP)], F32)
        d["uDe"] = self.dscr("uDe", [1024, S], BF16)
        d["cosS"] = self.dscr("cosS", [128, S], F32)
        d["sinS"] = self.dscr("sinS", [128, S], F32)
        self.d = d
        self.dbg_out = {}
        for nm in self.dbg:
            src = d[nm]
            self.dbg_out[nm] = nc.dram_tensor("dbg_" + nm, list(src.shape), src.dtype, kind="ExternalOutput").ap()

        with ExitStack() as st:
            self.semsets = [[st.enter_context(nc.semaphore("s%d_%d" % (a, j))) for j in range(NSEM)] for a in range(2)]
            T = lambda n, s, dt: st.enter_context(nc.sbuf_tensor(self.tn(n), s, dt))
            self.identb = T("identb", [128, 128], BF16)
            self.identf = T("identf_p", [128, 128], F32)
            self.onesb = T("onesb", [128, 128], BF16)
            self.mod = T("mod", [128, L, 48], F32)
            self.gmod = T("gmod", [128, L, 16], F32)
            self.nlam = T("nlam", [128, L], F32)
            self.gsub = T("gsub", [128, L], F32)
            self.bglu = T("bglu_s", [128, L, 16], F32)
            self.ssmD = T("ssmD_s", [128, L, 8], F32)
            self.fing = T("fing_s", [128, 16], F32)
            self.kvec = T("kvec_s", [128, len(KEXP)], F32)
            self.cact = T("cact_p", [128, 16], F32)
            self.normg = T("normg_p", [128, L, 16], F32)
            self.bada = T("bada_p", [128, L, 48], F32)
            with nc.Block() as blk:
                def clr(e):
                    for ss in self.semsets:
                        for s in ss:
                            e.sem_clear(s)
                blk.sync(clr)
            self.phase_init()
            if self.stop >= 1 and not os.environ.get('SKIPNORM'):
                self.phase_norm(0, d["xT"], final=False)
            for l in range(self.depth):
                if self.stop >= 2:
                    self.phase_inproj(l)
                if self.stop >= 3:
                    self.phase_s5_half(l, 0)
                    self.phase_attn(l)
                if self.stop >= 4:
                    self.phase_s5b(l)
                if self.stop >= 5:
                    self.phase_glu(l)
                if self.stop >= 6:
                    self.phase_outproj(l, d["xT"] if l == 0 else d["xres"])
                if l + 1 < self.depth:
                    self.phase_norm(l + 1, d["xres"], final=False)
            self.phase_norm(0, d["xres"], final=True)
            if self.dbg:
                self.phase_dbg()
        return nc

    def phase_dbg(self):
        nc = self.nc
        P = self.new_prog()
        with ExitStack() as st:
            for nm in self.dbg:
                src = self.d[nm]
                dst = self.dbg_out[nm]
                rows, cols = src.shape
                t = st.enter_context(nc.sbuf_tensor(self.tn("dbgt_" + nm), [128, rows // 128, cols], src.dtype))
                P.dma("sync", [(t[:], src.rearrange("(a p) c -> p a c", p=128))], (), ["t" + nm], "l" + nm)
                P.dma("sync", [(dst.rearrange("(a p) c -> p a c", p=128), t[:])], ["t" + nm], (), "s" + nm)
            P.finalize()

    def rr(self, P, out, x, tf, ti, kx, ko, ktmp, eng="vector"):
        P.ts(eng, tf, x, 1.0 / TWO_PI, None, ALU.mult, None, [kx], [ktmp + "f"])
        P.copy(eng, ti, tf, [ktmp + "f"], [ktmp + "i"])
        P.copy(eng, tf, ti, [ktmp + "i"], [ktmp + "f"])
        P.stt(out, tf, -TWO_PI, x, ALU.mult, ALU.add, [ktmp + "f", kx], [ko])
        P.ts(eng, out, out, math.pi, -math.pi, ALU.min, ALU.max, [ko], [ko])

    def phase_init(self):
        nc, d = self.nc, self.d
        L = DEPTH
        P = self.new_prog()
        with ExitStack() as st:
            T = lambda n, s, dt: st.enter_context(nc.sbuf_tensor(self.tn(n), s, dt))
            identf = self.identf
            P.memset("gpsimd", identf[:], 1.0, ["identf"])
            P.op("gpsimd", lambda e: e.affine_select(out=identf[:], in_=identf[:], pattern=[[1, 128]],
                                                     compare_op=ALU.is_equal, fill=0.0, base=0, channel_multiplier=-1),
                 ["identf"], ["identf"])
            P.copy("vector", self.identb[:], identf[:], ["identf"], ["identb"])
            P.memset("vector", self.onesb[:], 1.0, ["onesb"])
            c_l = T("c_ls", [128, 16], F32)
            cact = self.cact
            normg = self.normg
            bada = self.bada
            invf = T("invf_s", [128, 1], F32)
            subg = T("subg_s", [128, L], F32)
            lq = [T("lqs%d" % i, [128, L, 64], F32) for i in range(4)]
            P.dma("sync", [(c_l[:], d["c_l"]), (normg[:], d["normg"]), (bada[:], d["bada"]), (invf[:], d["invf"]),
                           (subg[:], d["subg"]), (self.bglu[:], d["bglu"]), (self.ssmD[:], d["ssmD"]),
                           (self.fing[:], d["fing"]), (self.kvec[:], d["kvec"]),
                           (lq[0][:], d["lq1"]), (lq[1][:], d["lk1"]), (lq[2][:], d["lq2"]), (lq[3][:], d["lk2"])],
                  (), ["small"], "small")
            P.act(cact[:], c_l[:], AF.Silu, ["small"], ["cact"])
            e12 = T("e12", [128, 2, L], F32)
            prod = T("lprod", [128, L, 64], F32)
            for i in range(2):
                P.tt("vector", prod[:], lq[2 * i][:], lq[2 * i + 1][:], ALU.mult, ["small"], ["lprod"])
                for l in range(L):
                    P.op("vector", lambda e, i=i, l=l: e.reduce_sum(out=e12[:, i, l:l + 1], in_=prod[:, l, :], axis=AX.X),
                         ["lprod"], ["e12"])
            P.act(e12[:], e12[:], AF.Exp, ["e12"], ["e12"])
            for l in range(L):
                li = 0.8 - 0.6 * math.exp(-0.3 * l)
                P.tt("vector", self.nlam[:, l:l + 1], e12[:, 1, l:l + 1], e12[:, 0, l:l + 1], ALU.subtract, ["e12"], ["nlam%d" % l])
                P.ts("vector", self.nlam[:, l:l + 1], self.nlam[:, l:l + 1], -li, None, ALU.add, None, ["nlam%d" % l], ["nlam%d" % l])
                P.ts("vector", self.gsub[:, l:l + 1], subg[:, l:l + 1], 1.0 - li, None, ALU.mult, None, ["small"], ["gsub%d" % l])
            wad = [T("wad%d" % i, [128, 16, 128], F32) for i in range(3)]
            psm = self.PS(st, "psm", [128, 48], F32)
            for _ in self.adaln_gen(P, 0, wad, psm):
                pass
            posi = T("posi", [128, S], I32)
            ang = T("ang", [128, S], F32)
            tf = T("rr_tf", [128, S], F32)
            ti = T("rr_ti", [128, S], I32)
            arg = T("rr_arg", [128, S], F32)
            P.dma("sync", [(posi[:], d["pos"].partition_broadcast(128))], (), ["posi"], "posi")
            P.copy("vector", ang[:], posi[:], ["posi"], ["ang"])
            P.ts("vector", ang[:], ang[:], invf[:, 0:1], None, ALU.mult, None, ["ang", "small"], ["ang"])
            self.rr(P, arg[:], ang[:], tf[:], ti[:], "ang", "arg", "rrt")
            P.act(tf[:], arg[:], AF.Sin, ["arg"], ["rrtf"])
            P.dma("sync", [(d["sinS"], tf[:])], ["rrtf"], (), "sinst")
            P.ts("vector", ang[:], ang[:], math.pi / 2, None, ALU.add, None, ["ang"], ["ang"])
            self.rr(P, arg[:], ang[:], tf[:], ti[:], "ang", "arg", "rrt")
            P.act(tf[:], arg[:], AF.Sin, ["arg"], ["rrtf"])
            P.dma("sync", [(d["cosS"], tf[:])], ["rrtf"], (), "cosst")
            P.finalize()

    def adaln_gen(self, P, l, wad, psm):
        d = self.d

        def ld(j):
            r = j % 3
            P.dma("sync", [(wad[r][:], d["w_ada"][l, :, j * 128:(j + 1) * 128].rearrange("(k p) c -> p k c", p=128))],
                  (), ["wad%d" % r], "wad%d" % r)
        ld(0)
        ld(1)
        for j in range(48):
            r = j % 3
            if j + 2 < 48:
                ld(j + 2)
            P.mm([(psm[:, j:j + 1], wad[r][:, k, :], self.cact[:, k:k + 1], k == 0, k == 15) for k in range(16)],
                 ["wad%d" % r, "cact"], ["psm"])
            yield
        P.tt("vector", self.mod[:, l, :], psm[:], self.bada[:, l, :], ALU.add, ["psm", "small"], ["mod%d" % l])
        P.stt(self.gmod[:, l, :], self.mod[:, l, 16:32], 1.0, self.normg[:, l, :], ALU.add, ALU.mult,
              ["mod%d" % l, "small"], ["gmod%d" % l])
        yield

    def phase_norm(self, l, src, final):
        nc, d = self.nc, self.d
        P = self.new_prog()
        with ExitStack() as st:
            T = lambda n, s, dt: st.enter_context(nc.sbuf_tensor(self.tn(n), s, dt))
            xt = [T("nx%d" % i, [128, 16, TW], F32) for i in range(2)]
            sqb = T("nsq", [128, 16, TW], BF16)
            ht = [T("nh%d" % i, [128, 16, TW], BF16) for i in range(2)]
            t1 = [T("nt1_%d" % i, [128, TW], F32) for i in range(2)]
            tmp = [T("ntmp%d" % i, [128, TW], F32) for i in range(3)]
            ps = [self.PS(st, "nps%d" % i, [128, TW], F32) for i in range(2)]
            srcv = src.rearrange("(k p) t -> p k t", p=128)
            dstv = (d["outT"] if final else d["hT"]).rearrange("(k p) t -> p k t", p=128)
            P.dma("sync", [(xt[0][:], srcv[:, :, 0:TW])], (), ["x0"], "x0")
            for n in range(NT):
                s = n % 2
                cs = slice(n * TW, (n + 1) * TW)
                if n + 1 < NT:
                    P.dma("sync", [(xt[1 - s][:], srcv[:, :, (n + 1) * TW:(n + 2) * TW])], (), ["x%d" % (1 - s)], "x%d" % (1 - s))
                P.act(sqb[:], xt[s][:], AF.Square, ["x%d" % s], ["sqb"])
                P.mm([(ps[s][:], self.onesb[:], sqb[:, k, :], k == 0, k == 15) for k in range(16)], ["sqb"], ["ps%d" % s])
                P.ts("vector", t1[s][:], ps[s][:], 1.0 / D, 1e-6, ALU.mult, ALU.add, ["ps%d" % s], ["t1%d" % s])
                P.act(t1[s][:], t1[s][:], AF.Sqrt, ["t1%d" % s], ["t1%d" % s])
                P.op("vector", lambda e, s=s: e.reciprocal(out=t1[s][:], in_=t1[s][:]), ["t1%d" % s], ["t1%d" % s])
                for k in range(16):
                    if final:
                        P.stt(xt[s][:, k, :], xt[s][:, k, :], self.fing[:, k:k + 1], t1[s][:], ALU.mult, ALU.mult,
                              ["x%d" % s, "t1%d" % s], ["x%d" % s])
                    else:
                        r = P.ring("ntmp", 3)
                        P.stt(tmp[r][:], xt[s][:, k, :], self.gmod[:, l, k:k + 1], t1[s][:], ALU.mult, ALU.mult,
                              ["x%d" % s, "t1%d" % s], ["tmp%d" % r])
                        P.act(ht[s][:, k, :], tmp[r][:], AF.Identity, ["tmp%d" % r], ["h%d" % s], bias=self.mod[:, l, k:k + 1])
                if final:
                    P.dma(STENG, [(dstv[:, :, cs], xt[s][:])], ["x%d" % s], (), "st%d" % s)
                else:
                    P.dma(STENG, [(dstv[:, :, cs], ht[s][:])], ["h%d" % s], (), "st%d" % s)
            P.finalize()

    def load_wchunk(self, P, Wd, KT, chunks, ci, wst, wbuf, slot, gkey):
        c0, kind = chunks[ci]
        r = P.ring("wst", 2)
        P.dma("sync", [(wst[r][:, 0:KT, :], Wd[:, c0:c0 + 256].rearrange("(k p) c -> p k c", p=128))],
              (), ["wst%d" % r], "wst%d" % r)
        dst = wbuf[slot][:, 0:KT, ci * 256:(ci + 1) * 256]
        if kind == "nat":
            P.copy(CASTENG, dst, wst[r][:, 0:KT, :], ["wst%d" % r], ["%s_%d" % (gkey, ci), gkey + "slot"])
        else:
            src5 = wst[r][:, 0:KT, :].rearrange("p k (i h j) -> p k i h j", i=4, h=2, j=32)
            for h in range(2):
                dsth = wbuf[slot][:, 0:KT, ci * 256 + h * 128: ci * 256 + (h + 1) * 128].rearrange("p k (i j) -> p k i j", i=4, j=32)
                P.copy(CASTENG, dsth, src5[:, :, :, h, :], ["wst%d" % r], ["%s_%d_%d" % (gkey, ci, h), gkey + "slot"])

    def load_wgroup(self, P, Wd, KT, chunks, wst, wbuf, slot, gkey):
        for ci in range(len(chunks)):
            self.load_wchunk(P, Wd, KT, chunks, ci, wst, wbuf, slot, gkey)

    def wkeys(self, gkey, chunks):
        ks = [gkey + "slot"]
        for ci, (c0, kind) in enumerate(chunks):
            if kind == "nat":
                ks.append("%s_%d" % (gkey, ci))
            else:
                ks += ["%s_%d_%d" % (gkey, ci, h) for h in range(2)]
        return ks

    def phase_inproj(self, l):
        nc, d = self.nc, self.d
        P = self.new_prog()
        Wd = d["w_in"][l]
        KT = 16
        groups = [("u", 0, "copy", d["uT"]), ("zs", 1024, "silu", d["zsT"]), ("q", 2048, "rope", d["qS"]),
                  ("k", 3072, "rope", d["kS"]), ("v", 4096, "tm", d["vS"]), ("za", 5120, "silu", d["zaT"])]
        with ExitStack() as st:
            T = lambda n, s, dt: st.enter_context(nc.sbuf_tensor(self.tn(n), s, dt))
            wst = [T("wst%d" % i, [128, KT, 256], F32) for i in range(2)]
            wbuf = [T("wbuf%d" % i, [128, KT, 1024], BF16) for i in range(2)]
            actt = [T("actt%d" % i, [128, KT, TW], BF16) for i in range(2)]
            stg = [T("stg%d" % i, [128, 8, TW], BF16) for i in range(2)]
            rtmp = [T("rtmp%d" % i, [128, 4, TW], F32) for i in range(2)]
            cosT = T("cosT", [128, S], F32)
            sinT = T("sinT", [128, S], F32)
            ps = [self.PS(st, "ips%d" % i, [128, TW], F32) for i in range(8)]
            P.dma("sync", [(cosT[:], d["cosS"]), (sinT[:], d["sinS"])], (), ["cs"], "cs")
            hv = d["hT"].rearrange("(k p) t -> p k t", p=128)
            groups = [groups[int(x)] for x in os.environ.get('KGROUPS', '0,1,2,3,4,5').split(',')]
            gchunks = [[(c0g + ci * 256, "rope" if mode == "rope" else "nat") for ci in range(4)] for (_, c0g, mode, _) in groups]
            iters = [(gi, n) for gi in range(len(groups)) for n in range(NT)]
            aslot = {}

            def emit_loads(idx):
                gi_, n_ = iters[idx]
                a_ = P.ring("actt", 2)
                aslot[idx] = a_
                P.dma("sync", [(actt[a_][:], hv[:, :, n_ * TW:(n_ + 1) * TW])], (), ["act%d" % a_], "act%d" % a_)
                if gi_ + 1 < len(groups) and 1 <= n_ <= 4:
                    self.load_wchunk(P, Wd, KT, gchunks[gi_ + 1], n_ - 1, wst, wbuf, (gi_ + 1) % 2, "w%d" % ((gi_ + 1) % 2))

            self.load_wgroup(P, Wd, KT, gchunks[0], wst, wbuf, 0, "w0")
            emit_loads(0)
            for idx, (gi, n) in enumerate(iters):
                (gname, c0g, mode, dst) = groups[gi]
                slot = gi % 2
                gkey = "w%d" % slot
                wk = self.wkeys(gkey, gchunks[gi])
                wb = wbuf[slot]
                if idx + 1 < len(iters):
                    emit_loads(idx + 1)
                if True:
                    a = aslot[idx]
                    cs = slice(n * TW, (n + 1) * TW)
                    sg = P.ring("stg", 2)
                    skey = "stg%d" % sg
                    if mode == "tm":
                        for tsub in range(4):
                            for cb in range(2):
                                b = P.ring("ps", 8)
                                P.mm([(ps[b][:], actt[a][:, k, tsub * 128:(tsub + 1) * 128], wb[:, k, cb * 512:(cb + 1) * 512],
                                       k == 0, k == KT - 1) for k in range(KT)], ["act%d" % a] + wk, ["ps%d" % b])
                                ov = stg[sg][:].rearrange("p a t -> p (a t)")[:, tsub * 1024 + cb * 512: tsub * 1024 + (cb + 1) * 512]
                                P.copy("scalar" if cb == 0 else "vector", ov, ps[b][:], ["ps%d" % b, skey], [skey + "_%d_%d" % (tsub, cb)])
                        dv = dst.rearrange("(ts p) c -> p ts c", p=128)[:, n * 4:(n + 1) * 4, :]
                        P.dma(STENG, [(dv, stg[sg][:].rearrange("p a t -> p (a t)").rearrange("p (ts c) -> p ts c", ts=4))],
                              [skey + "_%d_%d" % (tsub, cb) for tsub in range(4) for cb in range(2)], [skey], "sst%d" % sg)
                        continue
                    banks = []
                    for j in range(8):
                        b = P.ring("ps", 8)
                        banks.append(b)
                        P.mm([(ps[b][:], wb[:, k, j * 128:(j + 1) * 128], actt[a][:, k, :], k == 0, k == KT - 1) for k in range(KT)],
                             ["act%d" % a] + wk, ["ps%d" % b])
                        if mode == "copy":
                            P.copy("scalar" if j % 2 == 0 else "vector", stg[sg][:, j, :], ps[b][:], ["ps%d" % b, skey], [skey + "_%d" % j])
                        elif mode == "silu":
                            P.act(stg[sg][:, j, :], ps[b][:], AF.Silu, ["ps%d" % b, skey], [skey + "_%d" % j])
                        elif mode == "rope" and j % 2 == 1:
                            A, B = banks[j - 1], banks[j]
                            rt = P.ring("rtmp", 2)
                            rk = "rtmp%d" % rt
                            tmpv = rtmp[rt]
                            P.tt("vector", tmpv[:, 0, :], ps[A][:], cosT[:, cs], ALU.mult, ["ps%d" % A, "cs"], [rk + "a"])
                            P.tt("vector", tmpv[:, 1, :], ps[B][:], sinT[:, cs], ALU.mult, ["ps%d" % B, "cs"], [rk + "b"])
                            P.tt("vector", tmpv[:, 2, :], ps[B][:], cosT[:, cs], ALU.mult, ["ps%d" % B, "cs"], [rk + "c"])
                            P.tt("vector", tmpv[:, 3, :], ps[A][:], sinT[:, cs], ALU.mult, ["ps%d" % A, "cs"], [rk + "d"])
                            P.tt("gpsimd", stg[sg][:, j - 1, :], tmpv[:, 0, :], tmpv[:, 1, :], ALU.subtract, [rk + "a", rk + "b", skey], [skey + "_%d" % (j - 1)])
                            P.tt("gpsimd", stg[sg][:, j, :], tmpv[:, 2, :], tmpv[:, 3, :], ALU.add, [rk + "c", rk + "d", skey], [skey + "_%d" % j])
                    dv = dst.rearrange("(j p) t -> p j t", p=128)[:, :, cs]
                    P.dma(STENG, [(dv, stg[sg][:])], [skey + "_%d" % j for j in range(8)], [skey], "sst%d" % sg)
            P.finalize()

    def phase_attn(self, l):
        nc, d = self.nc, self.d
        P = self.new_prog()
        with ExitStack() as st:
            T = lambda n, s, dt: st.enter_context(nc.sbuf_tensor(self.tn(n), s, dt))
            qh = [T("qh%d" % i, [128, S], BF16) for i in range(2)]
            kz = [[T("kz%d_%d" % (m, i), [128, S], BF16) for i in range(2)] for m in range(2)]
            for m_ in range(2):
                for i_ in range(2):
                    P.memset("gpsimd" if m_ == 0 else "vector", kz[m_][i_][:], 0.0, ["hd%d" % i_])
            vh = [T("vh%d" % i, [128, 32, 128], BF16) for i in range(2)]
            zh = [T("zh%d" % i, [128, S], BF16) for i in range(2)]
            et = [T("et%d" % i, [128, TW], BF16) for i in range(6)]
            ft = [[T("aft%d_%d" % (a_, i), [128, TW], F32) for i in range(7)] for a_ in range(2)]
            sqb = T("asq", [128, TW], BF16)
            yst = [T("ayst%d" % i, [128, TW], BF16) for i in range(2)]
            sc = [self.PS(st, "asc%d" % i, [128, TW], F32) for i in range(4)]
            O = [self.PS(st, "aO%d" % i, [128, TW], F32) for i in range(2)]
            Lp = [self.PS(st, "aL%d" % i, [128, TW], F32) for i in range(2)]
            PAD = NCH // 2
            NKE = len(KEXP)
            pwr = T("a_pwr", [128, 32, NKE], F32); pwi = T("a_pwi", [128, 32, NKE], F32); npwi = T("a_npwi", [128, 32, NKE], F32)
            XA = [[T("a_XA%d_%d" % (q, i), [128, 2, PAD + NCH], F32) for i in range(2)] for q in range(4)]
            xpst = [T("a_xpst%d" % q, [128, 2, NCH], BF16) for q in range(4)]
            P.dma("sync", [(pwr[:].rearrange("p a b -> p (a b)"), d["PWs"][0:128, :]), (pwi[:].rearrange("p a b -> p (a b)"), d["PWs"][128:256, :]),
                           (npwi[:].rearrange("p a b -> p (a b)"), d["PWs"][256:384, :])], (), ["pw"], "pw")
            for q_ in range(4):
                for i_ in range(2):
                    P.memset("gpsimd", XA[q_][i_][:, :, 0:PAD], 0.0, ["Xpad%d_%d" % (q_, i_)])

            def scan_gen():
                for o in range(8):
                    for q in range(4):
                        pl = o * 4 + q
                        P.dma("sync", [(XA[q][0][:, :, PAD:], d["Sloc"][pl * 128:(pl + 1) * 128, :].rearrange("p (r c) -> p r c", r=2))],
                              (), ["X%d_0" % q], "sld%d" % q)
                        yield
                    cur = 0
                    for s_ in range(NSCAN):
                        sh = 1 << s_
                        for q in range(4):
                            pl = o * 4 + q
                            Ar = pwr[:, pl, LCH + s_: LCH + 1 + s_]
                            Ai = pwi[:, pl, LCH + s_: LCH + 1 + s_]
                            nAi = npwi[:, pl, LCH + s_: LCH + 1 + s_]
                            src, dst = XA[q][cur], XA[q][1 - cur]
                            sk = ["X%d_%d" % (q, cur), "Xpad%d_%d" % (q, cur), "pw"]
                            dk = "X%d_%d" % (q, 1 - cur)
                            P.stt(dst[:, 0, PAD:], src[:, 0, PAD - sh:PAD + NCH - sh], Ar, src[:, 0, PAD:], ALU.mult, ALU.add, sk, [dk, dk + "r"])
                            yield
                            P.stt(dst[:, 0, PAD:], src[:, 1, PAD - sh:PAD + NCH - sh], nAi, dst[:, 0, PAD:], ALU.mult, ALU.add, sk + [dk + "r"], [dk + "r"])
                            yield
                            P.stt(dst[:, 1, PAD:], src[:, 1, PAD - sh:PAD + NCH - sh], Ar, src[:, 1, PAD:], ALU.mult, ALU.add, sk, [dk + "i"])
                            yield
                            P.stt(dst[:, 1, PAD:], src[:, 0, PAD - sh:PAD + NCH - sh], Ai, dst[:, 1, PAD:], ALU.mult, ALU.add, sk + [dk + "i"], [dk + "i"])
                            P.lastw[dk] = P.lastw[dk + "i"]
                            P.readers[dk] = []
                            P.ops[P.lastw[dk + "i"]]["deps"].add(P.lastw[dk + "r"])
                            yield
                        cur = 1 - cur
                    for q in range(4):
                        pl = o * 4 + q
                        P.copy("vector", xpst[q][:], XA[q][cur][:, :, PAD - 1:PAD + NCH - 1], ["X%d_%d" % (q, cur), "Xpad%d_%d" % (q, cur)], ["xpst%d" % q])
                        yield
                        deferred.append((cur_unit[0] + 40, lambda q=q, pl=pl: P.dma(
                            STENG, [(d["Xps"][pl * 128:(pl + 1) * 128, :], xpst[q][:].rearrange("p r c -> p (r c)"))], ["xpst%d" % q], (), "xpd%d" % q)))
                        yield

            deferred = []
            cur_unit = [0]
            sgen = scan_gen()
            N_SCAN_OPS = 8 * (4 + NSCAN * 16 + 8)

            def head_loads(hd):
                hs = hd % 2
                pairs = []
                for which in range(2):
                    src = d["qS"] if which == 0 else d["kS"]
                    for m in range(2):
                        dst = qh[hs] if which == 0 else kz[m][hs]
                        pr = hd * 2 + m
                        Q, i = pr // 4, pr % 4
                        pairs.append((dst[m * 64: m * 64 + 32, :], src[(2 * Q) * 128 + i * 32:(2 * Q) * 128 + i * 32 + 32, :]))
                        pairs.append((dst[m * 64 + 32: m * 64 + 64, :], src[(2 * Q + 1) * 128 + i * 32:(2 * Q + 1) * 128 + i * 32 + 32, :]))
                pairs.append((vh[hs][:], d["vS"].rearrange("(ts p) c -> p ts c", p=128)[:, :, hd * 128:(hd + 1) * 128]))
                pairs.append((zh[hs][:], d["zaT"][hd * 128:(hd + 1) * 128, :]))
                P.dma("sync", pairs, (), ["hd%d" % hs], "hd%d" % hs)

            its = []
            for hd in range(8):
                for qi in range(NT):
                    nk = 4 * qi + 4
                    for kj in range(nk):
                        for m in range(2):
                            its.append((hd, qi, kj, m, nk))
            sbank = {}
            LOOK = 3

            def c0_of(qi, kj):
                return max(0, kj * 128 - qi * TW)

            def emit_S(i):
                hd, qi, kj, m, nk = its[i]
                hs = hd % 2
                b = P.ring("sc", 4)
                sbank[i] = b
                c0 = c0_of(qi, kj)
                P.mm([(sc[b][:, c0:], kz[m][hs][:, kj * 128:(kj + 1) * 128], qh[hs][:, qi * TW + c0:(qi + 1) * TW], True, True)],
                     ["hd%d" % hs], ["sc%d" % b])

            pend = []

            def epiA(hd, qi):
                a_ = P.ring("eset", 2)
                f = ft[a_]
                k = lambda n: "e%d_%s" % (a_, n)
                P.copy("vector", f[0][:], O[0][:], ["O0"], [k("o0")])
                P.act(f[2][:], Lp[0][:], AF.Ln, ["L0"], [k("r0")])
                P.copy("vector", f[1][:], O[1][:], ["O1"], [k("o1")])
                P.act(f[3][:], Lp[1][:], AF.Ln, ["L1"], [k("r1")])
                P.act(f[2][:], f[2][:], AF.Exp, [k("r0")], [k("r0")], scale=-1.0)
                P.act(f[3][:], f[3][:], AF.Exp, [k("r1")], [k("r1")], scale=-1.0)
                P.tt("vector", f[0][:], f[0][:], f[2][:], ALU.mult, [k("o0"), k("r0")], [k("o0")])
                P.tt("vector", f[1][:], f[1][:], f[3][:], ALU.mult, [k("o1"), k("r1")], [k("o1")])
                P.stt(f[4][:], f[1][:], self.nlam[:, l:l + 1], f[0][:], ALU.mult, ALU.add, [k("o0"), k("o1")], [k("d")])
                P.act(sqb[:], f[4][:], AF.Square, [k("d")], ["asq"])
                return a_

            def epiB(a_, hd, qi, pb):
                hs = hd % 2
                hk = "hd%d" % hs
                f = ft[a_]
                k = lambda n: "e%d_%s" % (a_, n)
                qs = slice(qi * TW, (qi + 1) * TW)
                P.mm([(sc[pb][:], self.onesb[:], sqb[:], True, True)], ["asq"], ["sc%d" % pb])
                P.act(f[5][:], sc[pb][:], AF.Ln, ["sc%d" % pb], [k("rs")], scale=1.0 / 128, bias=1e-6)
                P.act(f[5][:], f[5][:], AF.Exp, [k("rs")], [k("rs")], scale=-0.5)
                P.stt(f[6][:], f[4][:], self.gsub[:, l:l + 1], f[5][:], ALU.mult, ALU.mult, [k("d"), k("rs")], [k("y")])
                ys = P.ring("ayst", 2)
                P.tt("gpsimd", yst[ys][:], f[6][:], zh[hs][:, qs], ALU.mult, [k("y"), hk], ["yst%d" % ys])
                P.dma(STENG, [(d["yT"][1024 + hd * 128: 1024 + (hd + 1) * 128, qs], yst[ys][:])], ["yst%d" % ys], (), "ast%d" % ys)

            head_loads(0)
            for i in range(LOOK):
                emit_S(i)
            for i, (hd, qi, kj, m, nk) in enumerate(its):
                hs = hd % 2
                hk = "hd%d" % hs
                if qi == 0 and kj == 3 and m == 0 and hd + 1 < 8:
                    head_loads(hd + 1)
                if i + LOOK < len(its):
                    emit_S(i + LOOK)
                b = sbank[i]
                e_ = P.ring("et", 6)
                c0 = c0_of(qi, kj)
                P.act(et[e_][:, c0:], sc[b][:, c0:], AF.Exp, ["sc%d" % b], ["et%d" % e_], scale=0.125)
                if kj >= 4 * qi:
                    base = qi * TW + c0 - kj * 128
                    P.op("gpsimd", lambda e, e_=e_, base=base, c0=c0: e.affine_select(
                        out=et[e_][:, c0:], in_=et[e_][:, c0:], pattern=[[1, TW - c0]], compare_op=ALU.is_ge, fill=0.0,
                        base=base, channel_multiplier=-1), ["et%d" % e_], ["et%d" % e_])
                P.mm([(O[m][:, c0:], vh[hs][:, kj, :], et[e_][:, c0:], kj == 0, kj == nk - 1)], [hk, "et%d" % e_], ["O%d" % m])
                P.mm([(Lp[m][:, c0:], self.onesb[:], et[e_][:, c0:], kj == 0, kj == nk - 1)], ["et%d" % e_], ["L%d" % m])
                n_emit = (i + 1) * N_SCAN_OPS // (len(its) * 3 // 4) - i * N_SCAN_OPS // (len(its) * 3 // 4)
                cur_unit[0] = i
                for _ in range(n_emit):
                    next(sgen, None)
                while deferred and deferred[0][0] <= i:
                    deferred.pop(0)[1]()
                if pend and i >= pend[0][0]:
                    epiB(*pend.pop(0)[1], sbank[i])
                if kj == nk - 1 and m == 1:
                    a_ = epiA(hd, qi)
                    pend.append((i + 5, (a_, hd, qi)))
            while pend:
                epiB(*pend.pop(0)[1], sbank[len(its) - 1])
            for _ in sgen:
                pass
            while deferred:
                deferred.pop(0)[1]()
            P.finalize()

    def phase_s5(self, l):
        self.phase_s5_half(l, 0)

    def phase_s5_half(self, l, half):
        nc, d = self.nc, self.d
        P = self.new_prog()
        NP = 32
        NO = NP // 4
        p0 = half * NP
        NK = len(KEXP)
        with ExitStack() as st:
            T = lambda n, s, dt: st.enter_context(nc.sbuf_tensor(self.tn(n), s, dt))
            sh3 = [128, NP, NK]
            pwr = T("s_pwr", sh3, F32); pwi = T("s_pwi", sh3, F32); tf = T("s_tf", sh3, F32)
            WBr = T("s_WBr", [128, NP, LCH, 32], BF16); WBi = T("s_WBi", [128, NP, LCH, 32], BF16)
            WCr = T("s_WCr", [128, NP, LCH, 32], BF16); WCi = T("s_WCi", [128, NP, LCH, 32], BF16)
            Cpr = T("s_Cpr", [128, NP, 32], BF16); Cpn = T("s_Cpn", [128, NP, 32], BF16)
            KTt = T("s_KT", [128, NO, LCH, 128], BF16)
            st2 = ExitStack()
            T2 = lambda n, s_, dt: st2.enter_context(nc.sbuf_tensor(self.tn(n), s_, dt))
            Tmain = T
            T = T2
            are = T("s_are", [128, NP], F32); aim = T("s_aim", [128, NP], F32); lst = T("s_lst", [128, NP], F32)
            Bre = T("s_Bre", [128, NP, 16], F32); Bim = T("s_Bim", [128, NP, 16], F32)
            Cre = T("s_Cre", [128, NP, 16], F32); Cim = T("s_Cim", [128, NP, 16], F32)
            P.dma("sync", [(are[:], d["are"][l, :, p0:p0 + NP]), (aim[:], d["aim"][l, :, p0:p0 + NP]), (lst[:], d["lst"][l, :, p0:p0 + NP]),
                           (Bre[:], d["Bre"][l, :, p0:p0 + NP, :]), (Bim[:], d["Bim"][l, :, p0:p0 + NP, :]),
                           (Cre[:], d["Cre"][l, :, p0:p0 + NP, :]), (Cim[:], d["Cim"][l, :, p0:p0 + NP, :])], (), ["prm"], "prm")
            zr = T("s_zr", [128, NP], F32); zi = T("s_zi", [128, NP], F32)
            P.act(lst[:], lst[:], AF.Exp, ["prm"], ["step"])
            P.tt("vector", zr[:], are[:], lst[:], ALU.mult, ["prm", "step"], ["zr"])
            P.tt("vector", zi[:], aim[:], lst[:], ALU.mult, ["prm", "step"], ["zi"])
            kz = T("s_kz", sh3, F32); mag = T("s_mag", sh3, F32); argt = T("s_arg", sh3, F32)
            ti = T("s_ti", sh3, I32)
            kb = self.kvec[:].unsqueeze(1).broadcast_to(sh3)
            P.tt("vector", kz[:], zr[:].unsqueeze(2).broadcast_to(sh3), kb, ALU.mult, ["zr"], ["kz"])
            P.act(mag[:], kz[:], AF.Exp, ["kz"], ["mag"])
            P.tt("vector", kz[:], zi[:].unsqueeze(2).broadcast_to(sh3), kb, ALU.mult, ["zi", "mag"], ["kz"])
            self.rr(P, argt[:], kz[:], tf[:], ti[:], "kz", "arg", "rt")
            P.act(pwi[:], argt[:], AF.Sin, ["arg"], ["pwi"])
            P.tt("vector", pwi[:], pwi[:], mag[:], ALU.mult, ["pwi", "mag"], ["pwi"])
            P.ts("vector", kz[:], kz[:], math.pi / 2, None, ALU.add, None, ["kz", "pwi"], ["kz"])
            self.rr(P, argt[:], kz[:], tf[:], ti[:], "kz", "arg", "rt")
            P.act(pwr[:], argt[:], AF.Sin, ["arg"], ["pwr"])
            P.tt("vector", pwr[:], pwr[:], mag[:], ALU.mult, ["pwr", "mag"], ["pwr"])
            nr = T("s_nr", [128, NP], F32); den = T("s_den", [128, NP], F32); t0 = T("s_t0", [128, NP], F32)
            kr = T("s_kr", [128, NP], F32); ki = T("s_ki", [128, NP], F32)
            P.ts("vector", nr[:], pwr[:, :, 1], -1.0, None, ALU.add, None, ["pwr"], ["nr"])
            P.tt("vector", den[:], are[:], are[:], ALU.mult, ["prm"], ["den"])
            P.tt("vector", t0[:], aim[:], aim[:], ALU.mult, ["prm"], ["t0"])
            P.tt("vector", den[:], den[:], t0[:], ALU.add, ["den", "t0"], ["den"])
            P.op("vector", lambda e: e.reciprocal(out=den[:], in_=den[:]), ["den"], ["den"])
            P.tt("vector", kr[:], nr[:], are[:], ALU.mult, ["nr", "prm"], ["kr"])
            P.tt("vector", t0[:], pwi[:, :, 1], aim[:], ALU.mult, ["pwi", "prm", "den"], ["t0"])
            P.tt("vector", kr[:], kr[:], t0[:], ALU.add, ["kr", "t0"], ["kr"])
            P.tt("vector", kr[:], kr[:], den[:], ALU.mult, ["kr", "den"], ["kr"])
            P.tt("vector", ki[:], pwi[:, :, 1], are[:], ALU.mult, ["pwi", "prm"], ["ki"])
            P.tt("vector", t0[:], nr[:], aim[:], ALU.mult, ["nr", "prm", "kr"], ["t0"])
            P.tt("vector", ki[:], ki[:], t0[:], ALU.subtract, ["ki", "t0"], ["ki"])
            P.tt("vector", ki[:], ki[:], den[:], ALU.mult, ["ki", "den"], ["ki"])
            sh16 = [128, NP, 16]
            Bbr = T("s_Bbr", sh16, F32); Bbi = T("s_Bbi", sh16, F32); t16 = T("s_t16", sh16, F32)
            krb = kr[:].unsqueeze(2).broadcast_to(sh16); kib = ki[:].unsqueeze(2).broadcast_to(sh16)
            P.tt("vector", Bbr[:], Bre[:], krb, ALU.mult, ["prm", "kr"], ["Bbr"])
            P.tt("vector", t16[:], Bim[:], kib, ALU.mult, ["prm", "ki"], ["t16"])
            P.tt("vector", Bbr[:], Bbr[:], t16[:], ALU.subtract, ["Bbr", "t16"], ["Bbr"])
            P.tt("vector", Bbi[:], Bim[:], krb, ALU.mult, ["prm", "kr"], ["Bbi"])
            P.tt("vector", t16[:], Bre[:], kib, ALU.mult, ["prm", "ki", "Bbr"], ["t16"])
            P.tt("vector", Bbi[:], Bbi[:], t16[:], ALU.add, ["Bbi", "t16"], ["Bbi"])
            for i_, tbl in enumerate((WBr, WBi, WCr, WCi, Cpr, Cpn)):
                P.memset("gpsimd", tbl[:], 0.0, ["tbl%d" % i_])
            P.memset("gpsimd", KTt[:], 0.0, ["KT"])
            sh4 = [128, NP, LCH, 16]
            ta = T("s_ta", sh4, F32)
            tb = T("s_tb", sh4, F32)

            def pw(tbl, k0):
                return tbl[:, :, k0:k0 + LCH].unsqueeze(3).broadcast_to(sh4)

            def bc(tbl):
                return tbl[:, :, :].unsqueeze(2).broadcast_to(sh4)

            def place(dst_tbl, tblidx, name, neg=False):
                for g in range(2):
                    rs = slice(g * 64, (g + 1) * 64)
                    colsl = slice(g * 16, (g + 1) * 16)
                    if neg:
                        P.act(dst_tbl[rs, :, :, colsl], ta[rs], AF.Copy, ["ta", "tbl%d" % tblidx], ["%s%d" % (name, g)], scale=-1.0)
                    else:
                        P.copy("scalar" if g == 0 else "gpsimd", dst_tbl[rs, :, :, colsl], ta[rs], ["ta", "tbl%d" % tblidx], ["%s%d" % (name, g)])

            P.tt("vector", ta[:], pw(pwr, 0), bc(Bbr), ALU.mult, ["pwr", "Bbr"], ["ta"])
            P.tt("gpsimd", tb[:], pw(pwi, 0), bc(Bbi), ALU.mult, ["pwi", "Bbi"], ["tb"])
            P.tt("vector", ta[:], ta[:], tb[:], ALU.subtract, ["ta", "tb"], ["ta"])
            place(WBr, 0, "WBr")
            P.tt("vector", ta[:], pw(pwr, 0), bc(Bbi), ALU.mult, ["pwr", "Bbi", "WBr0", "WBr1"], ["ta"])
            P.tt("gpsimd", tb[:], pw(pwi, 0), bc(Bbr), ALU.mult, ["pwi", "Bbr", "ta"], ["tb"])
            P.tt("vector", ta[:], ta[:], tb[:], ALU.add, ["ta", "tb"], ["ta"])
            place(WBi, 1, "WBi")
            P.tt("vector", ta[:], pw(pwr, 1), bc(Cre), ALU.mult, ["pwr", "prm", "WBi0", "WBi1"], ["ta"])
            P.tt("gpsimd", tb[:], pw(pwi, 1), bc(Cim), ALU.mult, ["pwi", "prm", "ta"], ["tb"])
            P.tt("vector", ta[:], ta[:], tb[:], ALU.subtract, ["ta", "tb"], ["ta"])
            place(WCr, 2, "WCr")
            P.tt("vector", ta[:], pw(pwi, 1), bc(Cre), ALU.mult, ["pwi", "prm", "WCr0", "WCr1"], ["ta"])
            P.tt("gpsimd", tb[:], pw(pwr, 1), bc(Cim), ALU.mult, ["pwr", "prm", "ta"], ["tb"])
            P.tt("vector", ta[:], ta[:], tb[:], ALU.add, ["ta", "tb"], ["ta"])
            place(WCi, 3, "WCi", neg=True)
            for g in range(2):
                rs = slice(g * 64, (g + 1) * 64)
                colsl = slice(g * 16, (g + 1) * 16)
                P.copy("vector", Cpr[rs, :, colsl], Cre[rs], ["prm", "tbl4"], ["Cpr%d" % g])
                P.ts("vector", Cpn[rs, :, colsl], Cim[rs], -1.0, None, ALU.mult, None, ["prm", "tbl5"], ["Cpn%d" % g])
            P.ts("vector", tf[:], pwi[:], -1.0, None, ALU.mult, None, ["pwi", "pwr"], ["npwi"])
            T = Tmain
            WBk = ["WBr0", "WBr1", "WBi0", "WBi1"]
            WCk = ["WCr0", "WCr1", "WCi0", "WCi1"]
            pk = [self.PS(st, "s_pk%d" % i, [128, LCH, 32], F32) for i in range(2)]
            for pl in range(NP):
                o, q = pl // 4, pl % 4
                b = P.ring("pk", 2)
                rows = slice(32 * q, 32 * q + 32)
                items = []
                for lag in range(LCH):
                    tp = (0, 96) if q == 3 else None
                    items.append((pk[b][rows, lag, :], WBr[:, pl, lag, :], Cpr[:, pl, :], True, False, tp))
                    items.append((pk[b][rows, lag, :], WBi[:, pl, lag, :], Cpn[:, pl, :], False, True, tp))
                P.mm(items, WBk + ["Cpr0", "Cpr1", "Cpn0", "Cpn1"], ["pk%d" % b])
                P.copy("scalar", KTt[rows, o, :, 32 * q:32 * q + 32], pk[b][rows, :, :], ["pk%d" % b, "KT"], ["KTb%d" % pl])
            for o_ in range(NO):
                og_ = half * NO + o_
                P.stt(KTt[:, o_, 0, :], self.identf[:], self.ssmD[:, l, og_:og_ + 1], KTt[:, o_, 0, :], ALU.mult, ALU.add,
                      ["KTb%d" % (o_ * 4 + q_) for q_ in range(4)], ["KTd%d" % o_])
            KTk = ["KTb%d" % pl for pl in range(NP)] + ["KTd%d" % o_ for o_ in range(NO)]
            st2.close()
            P.fence()
            ur = T("s_ur", [128, S], BF16)
            uo = [T("s_uo%d" % i, [128, LCH, NCH], BF16) for i in range(2)]
            WBl = [T("s_WBl%d" % i, [128, 2, LCH, 128], BF16) for i in range(4)]
            sst = [T("s_sst%d" % i, [128, 2, NCH], F32) for i in range(3)]
            for q_ in range(4):
                P.memset("gpsimd", WBl[q_][:], 0.0, ["WBl%d_0" % q_, "WBl%d_1" % q_])
            pT = [self.PS(st, "s_pT%d" % i, [128, LCH, 128], BF16) for i in range(1)]
            pS = [self.PS(st, "s_pS%d" % i, [128, 2, NCH], F32) for i in range(2)]
            P.dma(STENG, [(d["WCs"][0:128, :], WCr[:].rearrange("p a b c -> p (a b c)")),
                          (d["WCs"][128:256, :], WCi[:].rearrange("p a b c -> p (a b c)")),
                          (d["KTs"], KTt[:].rearrange("p a b c -> p (a b c)")),
                          (d["PWs"][0:128, :], pwr[:].rearrange("p a b -> p (a b)")),
                          (d["PWs"][128:256, :], pwi[:].rearrange("p a b -> p (a b)")),
                          (d["PWs"][256:384, :], tf[:].rearrange("p a b -> p (a b)"))],
                  WCk + KTk + ["pwr", "pwi", "npwi"], (), "tblst")
            agen = iter(())
            if l + 1 < self.depth:
                wad = [T("s_wad%d" % i, [128, 16, 128], F32) for i in range(3)]
                psm = self.PS(st, "s_psm", [128, 48], F32)
                agen = self.adaln_gen(P, l + 1, wad, psm)
            for o in range(NO):
                og = o
                us = o % 2
                P.dma("sync", [(ur[:], d["uT"][og * 128:(og + 1) * 128, :])], (), ["ur"], "ur")
                P.copy("scalar", uo[us][:], ur[:].rearrange("p (c j) -> p j c", j=LCH), ["ur"], ["uo%d" % us])
                P.dma(STENG, [(d["uDe"][og * 128:(og + 1) * 128, :], uo[us][:].rearrange("p j c -> p (j c)"))], ["uo%d" % us], (), "udst%d" % us)
                for q in range(4):
                    pl = o * 4 + q
                    rows = slice(32 * q, 32 * q + 32)
                    tp = (0, 96) if q == 3 else None
                    for ri, tbl in enumerate((WBr, WBi)):
                        P.tr([(pT[0][rows, kk, :], tbl[:, pl, kk, :], tp) for kk in range(LCH)], self.identb[:], WBk, ["pT0"])
                        P.copy("scalar" if ri == 0 else "vector", WBl[q][rows, ri, :, :], pT[0][rows, :, :], ["pT0"], ["WBl%d_%d" % (q, ri)])
                    sb = P.ring("pS", 2)
                    items = []
                    for ri in range(2):
                        for j in range(LCH):
                            items.append((pS[sb][:, ri, :], WBl[q][:, ri, LCH - 1 - j, :], uo[us][:, j, :], j == 0, j == LCH - 1))
                    P.mm(items, ["WBl%d_0" % q, "WBl%d_1" % q, "uo%d" % us], ["pS%d" % sb])
                    r_ = P.ring("sst", 3)
                    P.copy("vector" if pl % 2 == 0 else "scalar", sst[r_][:], pS[sb][:], ["pS%d" % sb], ["sst%d" % r_])
                    P.dma(STENG, [(d["Sloc"][pl * 128:(pl + 1) * 128, :], sst[r_][:].rearrange("p r c -> p (r c)"))], ["sst%d" % r_], (), "sstd%d" % r_)
                    next(agen, None)
                    next(agen, None)
            for _ in agen:
                pass
            P.finalize()

    def phase_s5b(self, l):
        nc, d = self.nc, self.d
        P = self.new_prog()
        with ExitStack() as st:
            T = lambda n, s, dt: st.enter_context(nc.sbuf_tensor(self.tn(n), s, dt))
            WCr = T("b_WCr", [128, 32, LCH, 32], BF16); WCi = T("b_WCi", [128, 32, LCH, 32], BF16)
            KTt = T("b_KT", [128, 8, LCH, 128], BF16)
            uo = [T("b_uo%d" % i, [128, LCH, NCH], BF16) for i in range(2)]
            Xp = [[T("b_Xp%d_%d" % (ob, q), [128, 2, NCH], BF16) for q in range(4)] for ob in range(2)]
            yg = [T("b_yg%d" % i, [128, S], BF16) for i in range(2)]
            pY = [self.PS(st, "b_pY%d" % i, [128, NCH], F32) for i in range(4)]
            P.dma("sync", [(WCr[:].rearrange("p a b c -> p (a b c)"), d["WCs"][0:128, :]),
                           (WCi[:].rearrange("p a b c -> p (a b c)"), d["WCs"][128:256, :]),
                           (KTt[:].rearrange("p a b c -> p (a b c)"), d["KTs"])], (), ["tbl"], "tbl")

            def loads(o):
                us = o % 2
                pairs = [(uo[us][:].rearrange("p j c -> p (j c)"), d["uDe"][o * 128:(o + 1) * 128, :])]
                for q in range(4):
                    pl = o * 4 + q
                    pairs.append((Xp[us][q][:].rearrange("p r c -> p (r c)"), d["Xps"][pl * 128:(pl + 1) * 128, :]))
                P.dma("sync", pairs, (), ["in%d" % us], "in%d" % us)

            loads(0)
            for o in range(8):
                us = o % 2
                if o + 1 < 8:
                    loads(o + 1)
                uov = uo[us][:].rearrange("p j c -> p c j")
                ygv = yg[us][:].rearrange("p (c j) -> p c j", j=LCH)
                for i in range(LCH):
                    b = P.ring("pY", 4)
                    items = []
                    for j in range(i + 1):
                        items.append((pY[b][:], KTt[:, o, i - j, :], uov[:, :, j], j == 0, False))
                    for q in range(4):
                        pl = o * 4 + q
                        rows = slice(32 * q, 32 * q + 32)
                        tp = (0, 96) if q == 3 else None
                        items.append((pY[b][rows, :], WCr[:, pl, i, :], Xp[us][q][:, 0, :], False, False, tp))
                        items.append((pY[b][rows, :], WCi[:, pl, i, :], Xp[us][q][:, 1, :], False, q == 3, tp))
                    P.mm(items, ["tbl", "in%d" % us], ["pY%d" % b])
                    P.act(ygv[:, :, i], pY[b][:], AF.Gelu_apprx_tanh, ["pY%d" % b], ["yg%d_%d" % (us, i)])
                P.dma(STENG, [(d["ygT"][o * 128:(o + 1) * 128, :], yg[us][:])], ["yg%d_%d" % (us, i) for i in range(LCH)], (), "ygst%d" % us)
            P.finalize()

    def phase_glu(self, l):
        nc, d = self.nc, self.d
        P = self.new_prog()
        Wd = d["w_glu"][l]
        KT = 8
        with ExitStack() as st:
            T = lambda n, s, dt: st.enter_context(nc.sbuf_tensor(self.tn(n), s, dt))
            wst = [T("gwst%d" % i, [128, KT, 256], F32) for i in range(2)]
            wbuf = [T("gwbuf%d" % i, [128, KT, 1024], BF16) for i in range(2)]
            actt = [T("gact%d" % i, [128, KT, TW], BF16) for i in range(2)]
            zst = [T("gzs%d" % i, [128, 4, TW], BF16) for i in range(2)]
            stg = [T("gstg%d" % i, [128, 4, TW], BF16) for i in range(2)]
            sg_ = [T("gsig%d" % i, [128, TW], F32) for i in range(3)]
            tg_ = [T("gt%d" % i, [128, TW], F32) for i in range(3)]
            ps = [self.PS(st, "gps%d" % i, [128, TW], F32) for i in range(8)]
            av = d["ygT"].rearrange("(k p) t -> p k t", p=128)
            gchunks = [[(512 * gi, "nat"), (512 * gi + 256, "nat"), (1024 + 512 * gi, "nat"), (1024 + 512 * gi + 256, "nat")] for gi in range(2)]
            iters = [(gi, n) for gi in range(2) for n in range(NT)]
            aslot, zslot = {}, {}

            def emit_loads(idx):
                gi_, n_ = iters[idx]
                a_ = P.ring("actt", 2)
                z_ = P.ring("zs", 2)
                aslot[idx], zslot[idx] = a_, z_
                cs_ = slice(n_ * TW, (n_ + 1) * TW)
                P.dma("sync", [(actt[a_][:], av[:, :, cs_])], (), ["act%d" % a_], "act%d" % a_)
                P.dma("sync", [(zst[z_][:], d["zsT"].rearrange("(j p) t -> p j t", p=128)[:, gi_ * 4:(gi_ + 1) * 4, cs_])], (), ["zs%d" % z_], "zs%d" % z_)
                if gi_ + 1 < 2 and 1 <= n_ <= 4:
                    self.load_wchunk(P, Wd, KT, gchunks[gi_ + 1], n_ - 1, wst, wbuf, gi_ + 1, "w%d" % (gi_ + 1))

            self.load_wgroup(P, Wd, KT, gchunks[0], wst, wbuf, 0, "w0")
            emit_loads(0)
            for idx, (gi, n) in enumerate(iters):
                slot = gi
                gkey = "w%d" % slot
                wk = self.wkeys(gkey, gchunks[gi])
                wb = wbuf[slot]
                if idx + 1 < len(iters):
                    emit_loads(idx + 1)
                a, z = aslot[idx], zslot[idx]
                cs = slice(n * TW, (n + 1) * TW)
                sg = P.ring("stg", 2)
                for j in range(4):
                    bA = P.ring("ps", 8)
                    P.mm([(ps[bA][:], wb[:, k, j * 128:(j + 1) * 128], actt[a][:, k, :], k == 0, k == KT - 1) for k in range(KT)],
                         ["act%d" % a] + wk, ["ps%d" % bA])
                    bB = P.ring("ps", 8)
                    P.mm([(ps[bB][:], wb[:, k, 512 + j * 128:512 + (j + 1) * 128], actt[a][:, k, :], k == 0, k == KT - 1) for k in range(KT)],
                         ["act%d" % a] + wk, ["ps%d" % bB])
                    mt = gi * 4 + j
                    r = P.ring("sig", 3)
                    P.act(sg_[r][:], ps[bB][:], AF.Sigmoid, ["ps%d" % bB], ["sig%d" % r], bias=self.bglu[:, l, 8 + mt: 9 + mt])
                    P.stt(tg_[r][:], ps[bA][:], self.bglu[:, l, mt:mt + 1], sg_[r][:], ALU.add, ALU.mult, ["ps%d" % bA, "sig%d" % r], ["tg%d" % r])
                    P.tt("gpsimd", stg[sg][:, j, :], tg_[r][:], zst[z][:, j, :], ALU.mult, ["tg%d" % r, "zs%d" % z], ["stg%d_%d" % (sg, j)])
                dv = d["yT"].rearrange("(j p) t -> p j t", p=128)[:, gi * 4:(gi + 1) * 4, cs]
                P.dma(STENG, [(dv, stg[sg][:])], ["stg%d_%d" % (sg, j) for j in range(4)], (), "sst%d" % sg)
            P.finalize()

    def phase_outproj(self, l, xsrc):
        nc, d = self.nc, self.d
        P = self.new_prog()
        Wd = d["w_out"][l]
        KT = 16
        with ExitStack() as st:
            T = lambda n, s, dt: st.enter_context(nc.sbuf_tensor(self.tn(n), s, dt))
            wst = [T("owst%d" % i, [128, KT, 256], F32) for i in range(2)]
            wbuf = [T("owbuf%d" % i, [128, KT, 1024], BF16) for i in range(2)]
            actt = [T("oact%d" % i, [128, KT, TW], BF16) for i in range(2)]
            xo = [T("oxo%d" % i, [128, 8, TW], F32) for i in range(2)]
            ps = [self.PS(st, "ops%d" % i, [128, TW], F32) for i in range(8)]
            av = d["yT"].rearrange("(k p) t -> p k t", p=128)
            xv = xsrc.rearrange("(j p) t -> p j t", p=128)
            ov = d["xres"].rearrange("(j p) t -> p j t", p=128)
            gchunks = [[(1024 * gi + 256 * ci, "nat") for ci in range(4)] for gi in range(2)]
            iters = [(gi, n) for gi in range(2) for n in range(NT)]
            aslot, xslot = {}, {}

            def emit_loads(idx):
                gi_, n_ = iters[idx]
                a_ = P.ring("actt", 2)
                x__ = P.ring("xo", 2)
                aslot[idx], xslot[idx] = a_, x__
                cs_ = slice(n_ * TW, (n_ + 1) * TW)
                P.dma("sync", [(actt[a_][:], av[:, :, cs_])], (), ["act%d" % a_], "act%d" % a_)
                P.dma("sync", [(xo[x__][:], xv[:, gi_ * 8:(gi_ + 1) * 8, cs_])], (), ["xo%d" % x__], "xo%d" % x__)
                if gi_ + 1 < 2 and 1 <= n_ <= 4:
                    self.load_wchunk(P, Wd, KT, gchunks[gi_ + 1], n_ - 1, wst, wbuf, gi_ + 1, "w%d" % (gi_ + 1))

            self.load_wgroup(P, Wd, KT, gchunks[0], wst, wbuf, 0, "w0")
            emit_loads(0)
            for idx, (gi, n) in enumerate(iters):
                slot = gi
                gkey = "w%d" % slot
                wk = self.wkeys(gkey, gchunks[gi])
                wb = wbuf[slot]
                if idx + 1 < len(iters):
                    emit_loads(idx + 1)
                a, x_ = aslot[idx], xslot[idx]
                cs = slice(n * TW, (n + 1) * TW)
                for j in range(8):
                    b = P.ring("ps", 8)
                    P.mm([(ps[b][:], wb[:, k, j * 128:(j + 1) * 128], actt[a][:, k, :], k == 0, k == KT - 1) for k in range(KT)],
                         ["act%d" % a] + wk, ["ps%d" % b])
                    mt = gi * 8 + j
                    P.stt(xo[x_][:, j, :], ps[b][:], self.mod[:, l, 32 + mt:33 + mt], xo[x_][:, j, :], ALU.mult, ALU.add,
                          ["ps%d" % b, "xo%d" % x_], ["xn%d_%d" % (x_, j)])
                P.dma(STENG, [(ov[:, gi * 8:(gi + 1) * 8, cs], xo[x_][:])], ["xn%d_%d" % (x_, j) for j in range(8)], ["xo%d" % x_], "xst%d" % x_)
            P.finalize()


def _prep_shared(inp):
    L = DEPTH
    f = np.float32
    sh = {}
    sh["invf"] = np.tile((10000.0 ** (-np.arange(32, dtype=np.float64) / 32)).astype(f), 4).reshape(128, 1)
    sh["kvec"] = np.tile(np.asarray(KEXP, dtype=f)[None, :], (128, 1))
    sh["normg"] = np.ascontiguousarray(inp["norm_g"].reshape(L, 16, 128).transpose(2, 0, 1))
    sh["w_ada"] = inp["w_ada"]
    sh["bada"] = np.ascontiguousarray(inp["b_ada"].reshape(L, 48, 128).transpose(2, 0, 1))
    sh["w_in"] = inp["w_in"]
    sh["w_out"] = inp["w_out"]
    sh["w_glu"] = inp["w_glu"]
    sh["bglu"] = np.ascontiguousarray(inp["b_glu"].reshape(L, 16, 128).transpose(2, 0, 1))

    def st_layout(a):
        return np.ascontiguousarray(a.reshape(L, 32, 2, 64).transpose(0, 2, 3, 1).reshape(L, 128, 32))
    sh["are"] = st_layout(inp["ssm_a_re"])
    sh["aim"] = st_layout(inp["ssm_a_im"])
    sh["lst"] = st_layout(np.broadcast_to(inp["ssm_log_step"][:, :, None], (L, 64, 64)))

    def bc_layout(a):
        return np.ascontiguousarray(a.reshape(L, 32, 2, 64, 16).transpose(0, 2, 3, 1, 4).reshape(L, 128, 32, 16))
    sh["Bre"] = bc_layout(inp["ssm_b_re"])
    sh["Bim"] = bc_layout(inp["ssm_b_im"])
    sh["Cre"] = bc_layout(inp["ssm_c_re"].transpose(0, 1, 3, 2))
    sh["Cim"] = bc_layout(inp["ssm_c_im"].transpose(0, 1, 3, 2))
    sh["ssmD"] = np.ascontiguousarray(inp["ssm_d"].reshape(L, 8, 128).transpose(2, 0, 1))
    for a, b in (("lq1", "lam_q1"), ("lk1", "lam_k1"), ("lq2", "lam_q2"), ("lk2", "lam_k2")):
        sh[a] = np.ascontiguousarray(np.broadcast_to(inp[b][None, :, :], (128, L, 64)))
    sh["subg"] = np.ascontiguousarray(inp["sub_g"].T)
    sh["fing"] = np.ascontiguousarray(inp["final_g"].reshape(16, 128).T)
    return {k: np.ascontiguousarray(v) for k, v in sh.items()}


def _run(inp, depth=DEPTH, dbg=None, trace=False, stop=99, ncores=None):
    inp = {k: np.asarray(v) for k, v in inp.items()}
    B = ncores or inp["x"].shape[0]
    bld = Builder(depth=depth, dbg=dbg, stop=stop)
    nc = bld.build()
    sh = _prep_shared(inp)
    in_maps = []
    for b in range(B):
        m = dict(sh)
        m["xT"] = np.ascontiguousarray(inp["x"][b].T)
        m["c_l"] = np.ascontiguousarray(inp["c"][b].reshape(16, 128).T)
        m["pos"] = np.ascontiguousarray(inp["positions"][b].reshape(1, S).astype(np.int32))
        in_maps.append(m)
    res = run_bass_kernel_spmd(nc, in_maps, core_ids=list(range(B)), trace=trace)
    return res


def kernel(**inputs):
    res = _run(inputs)
    out = np.stack([np.ascontiguousarray(r["outT"].T) for r in res.results], axis=0)
    return out.astype(np.float32)
```

```python
import math
import numpy as np
from contextlib import ExitStack
import concourse.bass as bass
import concourse.mybir as mybir
from concourse.bass_utils import run_bass_kernel_spmd

F32 = mybir.dt.float32
BF16 = mybir.dt.bfloat16
I32 = mybir.dt.int32
AF = mybir.ActivationFunctionType
ALU = mybir.AluOpType
AX = mybir.AxisListType

S = 4096
D = 2048
DEPTH = 4
NT = 8
TW = 512
LCH = 8
NCH = S // LCH
NSCAN = 9
KEXP = list(range(0, LCH + 1)) + [LCH * (2 ** s) for s in range(1, NSCAN)]
NSEM = 50
TWO_PI = 2.0 * math.pi
import os
CASTENG = os.environ.get('CASTENG', 'gpsimd')
STENG = os.environ.get('STENG', 'sync')


class Prog:
    ENG = ("tensor", "vector", "scalar", "gpsimd", "sync")

    def __init__(self, nc, sems, clear):
        self.nc = nc
        self.sems = sems
        self.clear = clear
        self.ops = []
        self.lastw = {}
        self.readers = {}
        self.chan_last = {}
        self.chan_cnt = {}
        self.chan_eng = {}
        self.rings = {}
        self.fence_deps = set()
        self.fence_pending = set()

    def fence(self):
        last = {}
        for i, o in enumerate(self.ops):
            last[("c", o["chan"]) if o["chan"] is not None else ("e", o["eng"])] = i
        self.fence_deps = set(last.values())
        self.fence_pending = set(self.ENG)

    def ring(self, name, n):
        i = self.rings.get(name, 0)
        self.rings[name] = i + 1
        return i % n

    def op(self, eng, fn, reads=(), writes=(), chan=None, ndma=0):
        i = len(self.ops)
        deps = set()
        if eng in self.fence_pending:
            self.fence_pending.discard(eng)
            deps |= self.fence_deps
        for k in reads:
            if k in self.lastw:
                deps.add(self.lastw[k])
        for k in writes:
            if k in self.lastw:
                deps.add(self.lastw[k])
            deps.update(self.readers.get(k, ()))
        if chan is not None:
            if chan in self.chan_last:
                deps.add(self.chan_last[chan])
            self.chan_last[chan] = i
            self.chan_cnt[chan] = self.chan_cnt.get(chan, 0) + 16 * ndma
            self.chan_eng[chan] = eng
        self.ops.append(dict(eng=eng, fn=fn, deps=deps, chan=chan, ndma=ndma,
                             chan_val=self.chan_cnt.get(chan, 0) if chan is not None else None, sig=False))
        for k in reads:
            self.readers.setdefault(k, []).append(i)
        for k in writes:
            self.lastw[k] = i
            self.readers[k] = []
        return i

    def act(self, out, in_, func, reads, writes, **kw):
        self.op("scalar", lambda e: e.activation(out=out, in_=in_, func=func, **kw), reads, writes)

    def tt(self, eng, out, in0, in1, op, reads, writes):
        self.op(eng, lambda e: e.tensor_tensor(out=out, in0=in0, in1=in1, op=op), reads, writes)

    def ts(self, eng, out, in0, s1, s2, op0, op1, reads, writes):
        if s2 is None:
            self.op(eng, lambda e: e.tensor_scalar(out=out, in0=in0, scalar1=s1, scalar2=None, op0=op0), reads, writes)
        else:
            self.op(eng, lambda e: e.tensor_scalar(out=out, in0=in0, scalar1=s1, scalar2=s2, op0=op0, op1=op1), reads, writes)

    def stt(self, out, in0, scalar, in1, op0, op1, reads, writes):
        self.op("vector", lambda e: e.scalar_tensor_tensor(out=out, in0=in0, scalar=scalar, in1=in1, op0=op0, op1=op1),
                reads, writes)

    def copy(self, eng, out, in_, reads, writes):
        if eng == "scalar":
            self.op(eng, lambda e: e.activation(out=out, in_=in_, func=AF.Copy), reads, writes)
        else:
            self.op(eng, lambda e: e.tensor_copy(out=out, in_=in_), reads, writes)

    def memset(self, eng, ap, val, writes):
        self.op(eng, lambda e: e.memset(ap, val), (), writes)

    def mm(self, items, reads, writes):
        def f(e):
            r = None
            for it in items:
                (o, l, rh, st, sp) = it[:5]
                if len(it) > 5 and it[5] is not None:
                    r = e.matmul(o, lhsT=l, rhs=rh, start=st, stop=sp, tile_position=it[5])
                else:
                    r = e.matmul(o, lhsT=l, rhs=rh, start=st, stop=sp)
            return r
        self.op("tensor", f, reads, writes)

    def tr(self, items, ident, reads, writes):
        def f(e):
            r = None
            for it in items:
                (o, i) = it[:2]
                if len(it) > 2 and it[2] is not None:
                    r = e.transpose(out=o, in_=i, identity=ident, tile_position=it[2])
                else:
                    r = e.transpose(out=o, in_=i, identity=ident)
            return r
        self.op("tensor", f, reads, writes)

    def dma(self, eng, pairs, reads, writes, chan):
        self.op(eng, lambda e: [e.dma_start(out=o, in_=i) for (o, i) in pairs], reads, writes, chan=chan, ndma=len(pairs))

    def finalize(self):
        nc = self.nc
        ops = self.ops
        for o in ops:
            for d in o["deps"]:
                ops[d]["sig"] = True
        cnt = {e: 0 for e in self.ENG}
        for o in ops:
            if o["chan"] is None and o["sig"]:
                cnt[o["eng"]] += 1
                o["sigval"] = cnt[o["eng"]]
        assert len(self.ENG) + len(self.chan_cnt) <= len(self.sems), (len(self.chan_cnt), "too many dma channels")
        esem = {e: self.sems[j] for j, e in enumerate(self.ENG)}
        csem = {c: self.sems[len(self.ENG) + j] for j, c in enumerate(self.chan_cnt)}
        per = {e: [] for e in self.ENG}
        for o in ops:
            per[o["eng"]].append(o)

        def run(engname):
            def body(eng):
                if engname == "sync":
                    for s in self.clear:
                        eng.sem_clear(s)
                waited = {}
                for o in per[engname]:
                    need = {}
                    for d in o["deps"]:
                        p = ops[d]
                        if p["chan"] is not None:
                            s, v = csem[p["chan"]], p["chan_val"]
                        else:
                            if p["eng"] == engname and engname == "tensor":
                                continue
                            s, v = esem[p["eng"]], p["sigval"]
                        if need.get(s, (None, 0))[1] < v:
                            need[s] = (s, v)
                    for s, v in need.values():
                        if waited.get(s, 0) < v:
                            eng.wait_ge(s, v)
                            waited[s] = v
                    r = o["fn"](eng)
                    if o["chan"] is not None:
                        assert len(r) == o["ndma"]
                        for ins in r:
                            ins.then_inc(csem[o["chan"]], 16)
                    elif o["sig"]:
                        r.then_inc(esem[engname], 1)
                for c, e_ in self.chan_eng.items():
                    if e_ == engname:
                        eng.wait_ge(csem[c], self.chan_cnt[c])
            return body

        with nc.Block() as block:
            block.tensor(run("tensor"))
            block.vector(run("vector"))
            block.scalar(run("scalar"))
            block.gpsimd(run("gpsimd"))
            block.sync(run("sync"))


class Builder:
    def __init__(self, depth=DEPTH, dbg=None, stop=99):
        self.stop = stop
        self.depth = depth
        self.dbg = dbg or []
        self.nc = bass.Bass("TRN2", target_bir_lowering=False)
        self.phase_idx = 0

    def din(self, name, shape, dt=F32):
        return self.nc.dram_tensor(name, list(shape), dt, kind="ExternalInput").ap()

    def dscr(self, name, shape, dt):
        return self.nc.dram_tensor(name, list(shape), dt, kind="Internal").ap()

    def tn(self, n):
        return "%s_p%d" % (n, self.phase_idx)

    def PS(self, st, name, shape, dt):
        return st.enter_context(self.nc.psum_tensor(self.tn(name), shape, dt))

    def new_prog(self, dummy=True):
        if dummy:
            self.new_prog(dummy=False).finalize()
        k = self.phase_idx
        self.phase_idx += 1
        return Prog(self.nc, self.semsets[k % 2], self.semsets[(k + 1) % 2])

    def build(self):
        nc = self.nc
        L = DEPTH
        d = {}
        d["xT"] = self.din("xT", [D, S])
        d["c_l"] = self.din("c_l", [128, 16])
        d["pos"] = self.din("pos", [1, S], I32)
        d["invf"] = self.din("invf", [128, 1])
        d["kvec"] = self.din("kvec", [128, len(KEXP)])
        d["normg"] = self.din("normg", [128, L, 16])
        d["w_ada"] = self.din("w_ada", [L, D, 3 * D])
        d["bada"] = self.din("bada", [128, L, 48])
        d["w_in"] = self.din("w_in", [L, D, 6144])
        d["w_out"] = self.din("w_out", [L, D, D])
        d["w_glu"] = self.din("w_glu", [L, 1024, 2048])
        d["bglu"] = self.din("bglu", [128, L, 16])
        for nm in ("are", "aim", "lst"):
            d[nm] = self.din(nm, [L, 128, 32])
        for nm in ("Bre", "Bim", "Cre", "Cim"):
            d[nm] = self.din(nm, [L, 128, 32, 16])
        d["ssmD"] = self.din("ssmD", [128, L, 8])
        for nm in ("lq1", "lk1", "lq2", "lk2"):
            d[nm] = self.din(nm, [128, L, 64])
        d["subg"] = self.din("subg", [128, L])
        d["fing"] = self.din("fing", [128, 16])
        d["outT"] = nc.dram_tensor("outT", [D, S], F32, kind="ExternalOutput").ap()
        d["xres"] = self.dscr("xres", [D, S], F32)
        d["hT"] = self.dscr("hT", [D, S], BF16)
        d["uT"] = self.dscr("uT", [1024, S], BF16)
        d["zsT"] = self.dscr("zsT", [1024, S], BF16)
        d["qS"] = self.dscr("qS", [1024, S], BF16)
        d["kS"] = self.dscr("kS", [1024, S], BF16)
        d["vS"] = self.dscr("vS", [S, 1024], BF16)
        d["zaT"] = self.dscr("zaT", [1024, S], BF16)
        d["ygT"] = self.dscr("ygT", [1024, S], BF16)
        d["yT"] = self.dscr("yT", [D, S], BF16)
        d["Sloc"] = self.dscr("Sloc", [32 * 128, 2 * NCH], F32)
        d["Xps"] = self.dscr("Xps", [32 * 128, 2 * NCH], BF16)
        d["WCs"] = self.dscr("WCs", [2 * 128, 32 * LCH * 32], BF16)
        d["KTs"] = self.dscr("KTs", [128, 8 * LCH * 128], BF16)
        d["PWs"] = self.dscr("PWs", [3 * 128, 32 * len(KEXP)], F32)
        d["uDe"] = self.dscr("uDe", [1024, S], BF16)
        d["cosS"] = self.dscr("cosS", [128, S], F32)
        d["sinS"] = self.dscr("sinS", [128, S], F32)
        self.d = d
        self.dbg_out = {}
        for nm in self.dbg:
            src = d[nm]
            self.dbg_out[nm] = nc.dram_tensor("dbg_" + nm, list(src.shape), src.dtype, kind="ExternalOutput").ap()

        with ExitStack() as st:
            self.semsets = [[st.enter_context(nc.semaphore("s%d_%d" % (a, j))) for j in range(NSEM)] for a in range(2)]
            T = lambda n, s, dt: st.enter_context(nc.sbuf_tensor(self.tn(n), s, dt))
            self.identb = T("identb", [128, 128], BF16)
            self.identf = T("identf_p", [128, 128], F32)
            self.onesb = T("onesb", [128, 128], BF16)
            self.mod = T("mod", [128, L, 48], F32)
            self.gmod = T("gmod", [128, L, 16], F32)
            self.nlam = T("nlam", [128, L], F32)
            self.gsub = T("gsub", [128, L], F32)
            self.bglu = T("bglu_s", [128, L, 16], F32)
            self.ssmD = T("ssmD_s", [128, L, 8], F32)
            self.fing = T("fing_s", [128, 16], F32)
            self.kvec = T("kvec_s", [128, len(KEXP)], F32)
            self.cact = T("cact_p", [128, 16], F32)
            self.normg = T("normg_p", [128, L, 16], F32)
            self.bada = T("bada_p", [128, L, 48], F32)
            with nc.Block() as blk:
                def clr(e):
                    for ss in self.semsets:
                        for s in ss:
                            e.sem_clear(s)
                blk.sync(clr)
            self.phase_init()
            if self.stop >= 1 and not os.environ.get('SKIPNORM'):
                self.phase_norm(0, d["xT"], final=False)
            for l in range(self.depth):
                if self.stop >= 2:
                    self.phase_inproj(l)
                if self.stop >= 3:
                    self.phase_s5_half(l, 0)
                    self.phase_attn(l)
                if self.stop >= 4:
                    self.phase_s5b(l)
                if self.stop >= 5:
                    self.phase_glu(l)
                if self.stop >= 6:
                    self.phase_outproj(l, d["xT"] if l == 0 else d["xres"])
                if l + 1 < self.depth:
                    self.phase_norm(l + 1, d["xres"], final=False)
            self.phase_norm(0, d["xres"], final=True)
            if self.dbg:
                self.phase_dbg()
        return nc

    def phase_dbg(self):
        nc = self.nc
        P = self.new_prog()
        with ExitStack() as st:
            for nm in self.dbg:
                src = self.d[nm]
                dst = self.dbg_out[nm]
                rows, cols = src.shape
                t = st.enter_context(nc.sbuf_tensor(self.tn("dbgt_" + nm), [128, rows // 128, cols], src.dtype))
                P.dma("sync", [(t[:], src.rearrange("(a p) c -> p a c", p=128))], (), ["t" + nm], "l" + nm)
                P.dma("sync", [(dst.rearrange("(a p) c -> p a c", p=128), t[:])], ["t" + nm], (), "s" + nm)
            P.finalize()

    def rr(self, P, out, x, tf, ti, kx, ko, ktmp, eng="vector"):
        P.ts(eng, tf, x, 1.0 / TWO_PI, None, ALU.mult, None, [kx], [ktmp + "f"])
        P.copy(eng, ti, tf, [ktmp + "f"], [ktmp + "i"])
        P.copy(eng, tf, ti, [ktmp + "i"], [ktmp + "f"])
        P.stt(out, tf, -TWO_PI, x, ALU.mult, ALU.add, [ktmp + "f", kx], [ko])
        P.ts(eng, out, out, math.pi, -math.pi, ALU.min, ALU.max, [ko], [ko])

    def phase_init(self):
        nc, d = self.nc, self.d
        L = DEPTH
        P = self.new_prog()
        with ExitStack() as st:
            T = lambda n, s, dt: st.enter_context(nc.sbuf_tensor(self.tn(n), s, dt))
            identf = self.identf
            P.memset("gpsimd", identf[:], 1.0, ["identf"])
            P.op("gpsimd", lambda e: e.affine_select(out=identf[:], in_=identf[:], pattern=[[1, 128]],
                                                     compare_op=ALU.is_equal, fill=0.0, base=0, channel_multiplier=-1),
                 ["identf"], ["identf"])
            P.copy("vector", self.identb[:], identf[:], ["identf"], ["identb"])
            P.memset("vector", self.onesb[:], 1.0, ["onesb"])
            c_l = T("c_ls", [128, 16], F32)
            cact = self.cact
            normg = self.normg
            bada = self.bada
            invf = T("invf_s", [128, 1], F32)
            subg = T("subg_s", [128, L], F32)
            lq = [T("lqs%d" % i, [128, L, 64], F32) for i in range(4)]
            P.dma("sync", [(c_l[:], d["c_l"]), (normg[:], d["normg"]), (bada[:], d["bada"]), (invf[:], d["invf"]),
                           (subg[:], d["subg"]), (self.bglu[:], d["bglu"]), (self.ssmD[:], d["ssmD"]),
                           (self.fing[:], d["fing"]), (self.kvec[:], d["kvec"]),
                           (lq[0][:], d["lq1"]), (lq[1][:], d["lk1"]), (lq[2][:], d["lq2"]), (lq[3][:], d["lk2"])],
                  (), ["small"], "small")
            P.act(cact[:], c_l[:], AF.Silu, ["small"], ["cact"])
            e12 = T("e12", [128, 2, L], F32)
            prod = T("lprod", [128, L, 64], F32)
            for i in range(2):
                P.tt("vector", prod[:], lq[2 * i][:], lq[2 * i + 1][:], ALU.mult, ["small"], ["lprod"])
                for l in range(L):
                    P.op("vector", lambda e, i=i, l=l: e.reduce_sum(out=e12[:, i, l:l + 1], in_=prod[:, l, :], axis=AX.X),
                         ["lprod"], ["e12"])
            P.act(e12[:], e12[:], AF.Exp, ["e12"], ["e12"])
            for l in range(L):
                li = 0.8 - 0.6 * math.exp(-0.3 * l)
                P.tt("vector", self.nlam[:, l:l + 1], e12[:, 1, l:l + 1], e12[:, 0, l:l + 1], ALU.subtract, ["e12"], ["nlam%d" % l])
                P.ts("vector", self.nlam[:, l:l + 1], self.nlam[:, l:l + 1], -li, None, ALU.add, None, ["nlam%d" % l], ["nlam%d" % l])
                P.ts("vector", self.gsub[:, l:l + 1], subg[:, l:l + 1], 1.0 - li, None, ALU.mult, None, ["small"], ["gsub%d" % l])
            wad = [T("wad%d" % i, [128, 16, 128], F32) for i in range(3)]
            psm = self.PS(st, "psm", [128, 48], F32)
            for _ in self.adaln_gen(P, 0, wad, psm):
                pass
            posi = T("posi", [128, S], I32)
            ang = T("ang", [128, S], F32)
            tf = T("rr_tf", [128, S], F32)
            ti = T("rr_ti", [128, S], I32)
            arg = T("rr_arg", [128, S], F32)
            P.dma("sync", [(posi[:], d["pos"].partition_broadcast(128))], (), ["posi"], "posi")
            P.copy("vector", ang[:], posi[:], ["posi"], ["ang"])
            P.ts("vector", ang[:], ang[:], invf[:, 0:1], None, ALU.mult, None, ["ang", "small"], ["ang"])
            self.rr(P, arg[:], ang[:], tf[:], ti[:], "ang", "arg", "rrt")
            P.act(tf[:], arg[:], AF.Sin, ["arg"], ["rrtf"])
            P.dma("sync", [(d["sinS"], tf[:])], ["rrtf"], (), "sinst")
            P.ts("vector", ang[:], ang[:], math.pi / 2, None, ALU.add, None, ["ang"], ["ang"])
            self.rr(P, arg[:], ang[:], tf[:], ti[:], "ang", "arg", "rrt")
            P.act(tf[:], arg[:], AF.Sin, ["arg"], ["rrtf"])
            P.dma("sync", [(d["cosS"], tf[:])], ["rrtf"], (), "cosst")
            P.finalize()

    def adaln_gen(self, P, l, wad, psm):
        d = self.d

        def ld(j):
            r = j % 3
            P.dma("sync", [(wad[r][:], d["w_ada"][l, :, j * 128:(j + 1) * 128].rearrange("(k p) c -> p k c", p=128))],
                  (), ["wad%d" % r], "wad%d" % r)
        ld(0)
        ld(1)
        for j in range(48):
            r = j % 3
            if j + 2 < 48:
                ld(j + 2)
            P.mm([(psm[:, j:j + 1], wad[r][:, k, :], self.cact[:, k:k + 1], k == 0, k == 15) for k in range(16)],
                 ["wad%d" % r, "cact"], ["psm"])
            yield
        P.tt("vector", self.mod[:, l, :], psm[:], self.bada[:, l, :], ALU.add, ["psm", "small"], ["mod%d" % l])
        P.stt(self.gmod[:, l, :], self.mod[:, l, 16:32], 1.0, self.normg[:, l, :], ALU.add, ALU.mult,
              ["mod%d" % l, "small"], ["gmod%d" % l])
        yield

    def phase_norm(self, l, src, final):
        nc, d = self.nc, self.d
        P = self.new_prog()
        with ExitStack() as st:
            T = lambda n, s, dt: st.enter_context(nc.sbuf_tensor(self.tn(n), s, dt))
            xt = [T("nx%d" % i, [128, 16, TW], F32) for i in range(2)]
            sqb = T("nsq", [128, 16, TW], BF16)
            ht = [T("nh%d" % i, [128, 16, TW], BF16) for i in range(2)]
            t1 = [T("nt1_%d" % i, [128, TW], F32) for i in range(2)]
            tmp = [T("ntmp%d" % i, [128, TW], F32) for i in range(3)]
            ps = [self.PS(st, "nps%d" % i, [128, TW], F32) for i in range(2)]
            srcv = src.rearrange("(k p) t -> p k t", p=128)
            dstv = (d["outT"] if final else d["hT"]).rearrange("(k p) t -> p k t", p=128)
            P.dma("sync", [(xt[0][:], srcv[:, :, 0:TW])], (), ["x0"], "x0")
            for n in range(NT):
                s = n % 2
                cs = slice(n * TW, (n + 1) * TW)
                if n + 1 < NT:
                    P.dma("sync", [(xt[1 - s][:], srcv[:, :, (n + 1) * TW:(n + 2) * TW])], (), ["x%d" % (1 - s)], "x%d" % (1 - s))
                P.act(sqb[:], xt[s][:], AF.Square, ["x%d" % s], ["sqb"])
                P.mm([(ps[s][:], self.onesb[:], sqb[:, k, :], k == 0, k == 15) for k in range(16)], ["sqb"], ["ps%d" % s])
                P.act(t1[s][:], ps[s][:], AF.Ln, ["ps%d" % s], ["t1%d" % s], scale=1.0 / D, bias=1e-6)
                P.act(t1[s][:], t1[s][:], AF.Exp, ["t1%d" % s], ["t1%d" % s], scale=-0.5)
                for k in range(16):
                    if final:
                        P.stt(xt[s][:, k, :], xt[s][:, k, :], self.fing[:, k:k + 1], t1[s][:], ALU.mult, ALU.mult,
                              ["x%d" % s, "t1%d" % s], ["x%d" % s])
                    else:
                        r = P.ring("ntmp", 3)
                        P.stt(tmp[r][:], xt[s][:, k, :], self.gmod[:, l, k:k + 1], t1[s][:], ALU.mult, ALU.mult,
                              ["x%d" % s, "t1%d" % s], ["tmp%d" % r])
                        P.act(ht[s][:, k, :], tmp[r][:], AF.Identity, ["tmp%d" % r], ["h%d" % s], bias=self.mod[:, l, k:k + 1])
                if final:
                    P.dma(STENG, [(dstv[:, :, cs], xt[s][:])], ["x%d" % s], (), "st%d" % s)
                else:
                    P.dma(STENG, [(dstv[:, :, cs], ht[s][:])], ["h%d" % s], (), "st%d" % s)
            P.finalize()

    def load_wchunk(self, P, Wd, KT, chunks, ci, wst, wbuf, slot, gkey):
        c0, kind = chunks[ci]
        r = P.ring("wst", 2)
        P.dma("sync", [(wst[r][:, 0:KT, :], Wd[:, c0:c0 + 256].rearrange("(k p) c -> p k c", p=128))],
              (), ["wst%d" % r], "wst%d" % r)
        dst = wbuf[slot][:, 0:KT, ci * 256:(ci + 1) * 256]
        if kind == "nat":
            P.copy(CASTENG, dst, wst[r][:, 0:KT, :], ["wst%d" % r], ["%s_%d" % (gkey, ci), gkey + "slot"])
        else:
            src5 = wst[r][:, 0:KT, :].rearrange("p k (i h j) -> p k i h j", i=4, h=2, j=32)
            for h in range(2):
                dsth = wbuf[slot][:, 0:KT, ci * 256 + h * 128: ci * 256 + (h + 1) * 128].rearrange("p k (i j) -> p k i j", i=4, j=32)
                P.copy(CASTENG, dsth, src5[:, :, :, h, :], ["wst%d" % r], ["%s_%d_%d" % (gkey, ci, h), gkey + "slot"])

    def load_wgroup(self, P, Wd, KT, chunks, wst, wbuf, slot, gkey):
        for ci in range(len(chunks)):
            self.load_wchunk(P, Wd, KT, chunks, ci, wst, wbuf, slot, gkey)

    def wkeys(self, gkey, chunks):
        ks = [gkey + "slot"]
        for ci, (c0, kind) in enumerate(chunks):
            if kind == "nat":
                ks.append("%s_%d" % (gkey, ci))
            else:
                ks += ["%s_%d_%d" % (gkey, ci, h) for h in range(2)]
        return ks

    def phase_inproj(self, l):
        nc, d = self.nc, self.d
        P = self.new_prog()
        Wd = d["w_in"][l]
        KT = 16
        groups = [("u", 0, "copy", d["uT"]), ("zs", 1024, "silu", d["zsT"]), ("q", 2048, "rope", d["qS"]),
                  ("k", 3072, "rope", d["kS"]), ("v", 4096, "tm", d["vS"]), ("za", 5120, "silu", d["zaT"])]
        with ExitStack() as st:
            T = lambda n, s, dt: st.enter_context(nc.sbuf_tensor(self.tn(n), s, dt))
            wst = [T("wst%d" % i, [128, KT, 256], F32) for i in range(2)]
            wbuf = [T("wbuf%d" % i, [128, KT, 1024], BF16) for i in range(2)]
            actt = [T("actt%d" % i, [128, KT, TW], BF16) for i in range(2)]
            stg = [T("stg%d" % i, [128, 8, TW], BF16) for i in range(2)]
            rtmp = [T("rtmp%d" % i, [128, 4, TW], F32) for i in range(2)]
            cosT = T("cosT", [128, S], F32)
            sinT = T("sinT", [128, S], F32)
            ps = [self.PS(st, "ips%d" % i, [128, TW], F32) for i in range(8)]
            P.dma("sync", [(cosT[:], d["cosS"]), (sinT[:], d["sinS"])], (), ["cs"], "cs")
            hv = d["hT"].rearrange("(k p) t -> p k t", p=128)
            groups = [groups[int(x)] for x in os.environ.get('KGROUPS', '0,1,2,3,4,5').split(',')]
            gchunks = [[(c0g + ci * 256, "rope" if mode == "rope" else "nat") for ci in range(4)] for (_, c0g, mode, _) in groups]
            iters = [(gi, n) for gi in range(len(groups)) for n in range(NT)]
            aslot = {}

            def emit_loads(idx):
                gi_, n_ = iters[idx]
                a_ = P.ring("actt", 2)
                aslot[idx] = a_
                P.dma("sync", [(actt[a_][:], hv[:, :, n_ * TW:(n_ + 1) * TW])], (), ["act%d" % a_], "act%d" % a_)
                if gi_ + 1 < len(groups) and 1 <= n_ <= 4:
                    self.load_wchunk(P, Wd, KT, gchunks[gi_ + 1], n_ - 1, wst, wbuf, (gi_ + 1) % 2, "w%d" % ((gi_ + 1) % 2))

            self.load_wgroup(P, Wd, KT, gchunks[0], wst, wbuf, 0, "w0")
            emit_loads(0)
            for idx, (gi, n) in enumerate(iters):
                (gname, c0g, mode, dst) = groups[gi]
                slot = gi % 2
                gkey = "w%d" % slot
                wk = self.wkeys(gkey, gchunks[gi])
                wb = wbuf[slot]
                if idx + 1 < len(iters):
                    emit_loads(idx + 1)
                if True:
                    a = aslot[idx]
                    cs = slice(n * TW, (n + 1) * TW)
                    sg = P.ring("stg", 2)
                    skey = "stg%d" % sg
                    if mode == "tm":
                        for tsub in range(4):
                            for cb in range(2):
                                b = P.ring("ps", 8)
                                P.mm([(ps[b][:], actt[a][:, k, tsub * 128:(tsub + 1) * 128], wb[:, k, cb * 512:(cb + 1) * 512],
                                       k == 0, k == KT - 1) for k in range(KT)], ["act%d" % a] + wk, ["ps%d" % b])
                                ov = stg[sg][:].rearrange("p a t -> p (a t)")[:, tsub * 1024 + cb * 512: tsub * 1024 + (cb + 1) * 512]
                                P.copy("scalar" if cb == 0 else "vector", ov, ps[b][:], ["ps%d" % b, skey], [skey + "_%d_%d" % (tsub, cb)])
                        dv = dst.rearrange("(ts p) c -> p ts c", p=128)[:, n * 4:(n + 1) * 4, :]
                        P.dma(STENG, [(dv, stg[sg][:].rearrange("p a t -> p (a t)").rearrange("p (ts c) -> p ts c", ts=4))],
                              [skey + "_%d_%d" % (tsub, cb) for tsub in range(4) for cb in range(2)], [skey], "sst%d" % sg)
                        continue
                    banks = []
                    for j in range(8):
                        b = P.ring("ps", 8)
                        banks.append(b)
                        P.mm([(ps[b][:], wb[:, k, j * 128:(j + 1) * 128], actt[a][:, k, :], k == 0, k == KT - 1) for k in range(KT)],
                             ["act%d" % a] + wk, ["ps%d" % b])
                        if mode == "copy":
                            P.copy("scalar" if j % 2 == 0 else "vector", stg[sg][:, j, :], ps[b][:], ["ps%d" % b, skey], [skey + "_%d" % j])
                        elif mode == "silu":
                            P.act(stg[sg][:, j, :], ps[b][:], AF.Silu, ["ps%d" % b, skey], [skey + "_%d" % j])
                        elif mode == "rope" and j % 2 == 1:
                            A, B = banks[j - 1], banks[j]
                            rt = P.ring("rtmp", 2)
                            rk = "rtmp%d" % rt
                            tmpv = rtmp[rt]
                            P.tt("vector", tmpv[:, 0, :], ps[A][:], cosT[:, cs], ALU.mult, ["ps%d" % A, "cs"], [rk + "a"])
                            P.tt("vector", tmpv[:, 1, :], ps[B][:], sinT[:, cs], ALU.mult, ["ps%d" % B, "cs"], [rk + "b"])
                            P.tt("vector", tmpv[:, 2, :], ps[B][:], cosT[:, cs], ALU.mult, ["ps%d" % B, "cs"], [rk + "c"])
                            P.tt("vector", tmpv[:, 3, :], ps[A][:], sinT[:, cs], ALU.mult, ["ps%d" % A, "cs"], [rk + "d"])
                            P.tt("gpsimd", stg[sg][:, j - 1, :], tmpv[:, 0, :], tmpv[:, 1, :], ALU.subtract, [rk + "a", rk + "b", skey], [skey + "_%d" % (j - 1)])
                            P.tt("gpsimd", stg[sg][:, j, :], tmpv[:, 2, :], tmpv[:, 3, :], ALU.add, [rk + "c", rk + "d", skey], [skey + "_%d" % j])
                    dv = dst.rearrange("(j p) t -> p j t", p=128)[:, :, cs]
                    P.dma(STENG, [(dv, stg[sg][:])], [skey + "_%d" % j for j in range(8)], [skey], "sst%d" % sg)
            P.finalize()

    def phase_attn(self, l):
        nc, d = self.nc, self.d
        P = self.new_prog()
        with ExitStack() as st:
            T = lambda n, s, dt: st.enter_context(nc.sbuf_tensor(self.tn(n), s, dt))
            qh = [T("qh%d" % i, [128, S], BF16) for i in range(2)]
            kz = [[T("kz%d_%d" % (m, i), [128, S], BF16) for i in range(2)] for m in range(2)]
            for m_ in range(2):
                for i_ in range(2):
                    P.memset("gpsimd" if m_ == 0 else "vector", kz[m_][i_][:], 0.0, ["hd%d" % i_])
            vh = [T("vh%d" % i, [128, 32, 128], BF16) for i in range(2)]
            zh = [T("zh%d" % i, [128, S], BF16) for i in range(2)]
            et = [T("et%d" % i, [128, TW], BF16) for i in range(6)]
            ft = [[T("aft%d_%d" % (a_, i), [128, TW], F32) for i in range(7)] for a_ in range(2)]
            sqb = T("asq", [128, TW], BF16)
            yst = [T("ayst%d" % i, [128, TW], BF16) for i in range(2)]
            sc = [self.PS(st, "asc%d" % i, [128, TW], F32) for i in range(4)]
            O = [self.PS(st, "aO%d" % i, [128, TW], F32) for i in range(2)]
            Lp = [self.PS(st, "aL%d" % i, [128, TW], F32) for i in range(2)]
            PAD = NCH // 2
            NKE = len(KEXP)
            pwr = T("a_pwr", [128, 32, NKE], F32); pwi = T("a_pwi", [128, 32, NKE], F32); npwi = T("a_npwi", [128, 32, NKE], F32)
            XA = [[T("a_XA%d_%d" % (q, i), [128, 2, PAD + NCH], F32) for i in range(2)] for q in range(4)]
            xpst = [T("a_xpst%d" % q, [128, 2, NCH], BF16) for q in range(4)]
            P.dma("sync", [(pwr[:].rearrange("p a b -> p (a b)"), d["PWs"][0:128, :]), (pwi[:].rearrange("p a b -> p (a b)"), d["PWs"][128:256, :]),
                           (npwi[:].rearrange("p a b -> p (a b)"), d["PWs"][256:384, :])], (), ["pw"], "pw")
            for q_ in range(4):
                for i_ in range(2):
                    P.memset("gpsimd", XA[q_][i_][:, :, 0:PAD], 0.0, ["Xpad%d_%d" % (q_, i_)])

            def scan_gen():
                for o in range(8):
                    for q in range(4):
                        pl = o * 4 + q
                        P.dma("sync", [(XA[q][0][:, :, PAD:], d["Sloc"][pl * 128:(pl + 1) * 128, :].rearrange("p (r c) -> p r c", r=2))],
                              (), ["X%d_0" % q], "sld%d" % q)
                        yield
                    cur = 0
                    for s_ in range(NSCAN):
                        sh = 1 << s_
                        for q in range(4):
                            pl = o * 4 + q
                            Ar = pwr[:, pl, LCH + s_: LCH + 1 + s_]
                            Ai = pwi[:, pl, LCH + s_: LCH + 1 + s_]
                            nAi = npwi[:, pl, LCH + s_: LCH + 1 + s_]
                            src, dst = XA[q][cur], XA[q][1 - cur]
                            sk = ["X%d_%d" % (q, cur), "Xpad%d_%d" % (q, cur), "pw"]
                            dk = "X%d_%d" % (q, 1 - cur)
                            P.stt(dst[:, 0, PAD:], src[:, 0, PAD - sh:PAD + NCH - sh], Ar, src[:, 0, PAD:], ALU.mult, ALU.add, sk, [dk, dk + "r"])
                            yield
                            P.stt(dst[:, 0, PAD:], src[:, 1, PAD - sh:PAD + NCH - sh], nAi, dst[:, 0, PAD:], ALU.mult, ALU.add, sk + [dk + "r"], [dk + "r"])
                            yield
                            P.stt(dst[:, 1, PAD:], src[:, 1, PAD - sh:PAD + NCH - sh], Ar, src[:, 1, PAD:], ALU.mult, ALU.add, sk, [dk + "i"])
                            yield
                            P.stt(dst[:, 1, PAD:], src[:, 0, PAD - sh:PAD + NCH - sh], Ai, dst[:, 1, PAD:], ALU.mult, ALU.add, sk + [dk + "i"], [dk + "i"])
                            P.lastw[dk] = P.lastw[dk + "i"]
                            P.readers[dk] = []
                            P.ops[P.lastw[dk + "i"]]["deps"].add(P.lastw[dk + "r"])
                            yield
                        cur = 1 - cur
                    for q in range(4):
                        pl = o * 4 + q
                        P.copy("vector", xpst[q][:], XA[q][cur][:, :, PAD - 1:PAD + NCH - 1], ["X%d_%d" % (q, cur), "Xpad%d_%d" % (q, cur)], ["xpst%d" % q])
                        yield
                        deferred.append((cur_unit[0] + 40, lambda q=q, pl=pl: P.dma(
                            STENG, [(d["Xps"][pl * 128:(pl + 1) * 128, :], xpst[q][:].rearrange("p r c -> p (r c)"))], ["xpst%d" % q], (), "xpd%d" % q)))
                        yield

            deferred = []
            cur_unit = [0]
            sgen = scan_gen()
            N_SCAN_OPS = 8 * (4 + NSCAN * 16 + 8)

            def head_loads(hd):
                hs = hd % 2
                pairs = []
                for which in range(2):
                    src = d["qS"] if which == 0 else d["kS"]
                    for m in range(2):
                        dst = qh[hs] if which == 0 else kz[m][hs]
                        pr = hd * 2 + m
                        Q, i = pr // 4, pr % 4
                        pairs.append((dst[m * 64: m * 64 + 32, :], src[(2 * Q) * 128 + i * 32:(2 * Q) * 128 + i * 32 + 32, :]))
                        pairs.append((dst[m * 64 + 32: m * 64 + 64, :], src[(2 * Q + 1) * 128 + i * 32:(2 * Q + 1) * 128 + i * 32 + 32, :]))
                pairs.append((vh[hs][:], d["vS"].rearrange("(ts p) c -> p ts c", p=128)[:, :, hd * 128:(hd + 1) * 128]))
                pairs.append((zh[hs][:], d["zaT"][hd * 128:(hd + 1) * 128, :]))
                P.dma("sync", pairs, (), ["hd%d" % hs], "hd%d" % hs)

            its = []
            for hd in range(8):
                for qi in range(NT):
                    nk = 4 * qi + 4
                    for kj in range(nk):
                        for m in range(2):
                            its.append((hd, qi, kj, m, nk))
            sbank = {}
            LOOK = 3

            def c0_of(qi, kj):
                return max(0, kj * 128 - qi * TW)

            def emit_S(i):
                hd, qi, kj, m, nk = its[i]
                hs = hd % 2
                b = P.ring("sc", 4)
                sbank[i] = b
                c0 = c0_of(qi, kj)
                P.mm([(sc[b][:, c0:], kz[m][hs][:, kj * 128:(kj + 1) * 128], qh[hs][:, qi * TW + c0:(qi + 1) * TW], True, True)],
                     ["hd%d" % hs], ["sc%d" % b])

            pend = []

            def epiA(hd, qi):
                a_ = P.ring("eset", 2)
                f = ft[a_]
                k = lambda n: "e%d_%s" % (a_, n)
                P.copy("vector", f[0][:], O[0][:], ["O0"], [k("o0")])
                P.act(f[2][:], Lp[0][:], AF.Ln, ["L0"], [k("r0")])
                P.copy("vector", f[1][:], O[1][:], ["O1"], [k("o1")])
                P.act(f[3][:], Lp[1][:], AF.Ln, ["L1"], [k("r1")])
                P.act(f[2][:], f[2][:], AF.Exp, [k("r0")], [k("r0")], scale=-1.0)
                P.act(f[3][:], f[3][:], AF.Exp, [k("r1")], [k("r1")], scale=-1.0)
                P.tt("vector", f[0][:], f[0][:], f[2][:], ALU.mult, [k("o0"), k("r0")], [k("o0")])
                P.tt("vector", f[1][:], f[1][:], f[3][:], ALU.mult, [k("o1"), k("r1")], [k("o1")])
                P.stt(f[4][:], f[1][:], self.nlam[:, l:l + 1], f[0][:], ALU.mult, ALU.add, [k("o0"), k("o1")], [k("d")])
                P.act(sqb[:], f[4][:], AF.Square, [k("d")], ["asq"])
                return a_

            def epiB(a_, hd, qi, pb):
                hs = hd % 2
                hk = "hd%d" % hs
                f = ft[a_]
                k = lambda n: "e%d_%s" % (a_, n)
                qs = slice(qi * TW, (qi + 1) * TW)
                P.mm([(sc[pb][:], self.onesb[:], sqb[:], True, True)], ["asq"], ["sc%d" % pb])
                P.act(f[5][:], sc[pb][:], AF.Ln, ["sc%d" % pb], [k("rs")], scale=1.0 / 128, bias=1e-6)
                P.act(f[5][:], f[5][:], AF.Exp, [k("rs")], [k("rs")], scale=-0.5)
                P.stt(f[6][:], f[4][:], self.gsub[:, l:l + 1], f[5][:], ALU.mult, ALU.mult, [k("d"), k("rs")], [k("y")])
                ys = P.ring("ayst", 2)
                P.tt("gpsimd", yst[ys][:], f[6][:], zh[hs][:, qs], ALU.mult, [k("y"), hk], ["yst%d" % ys])
                P.dma(STENG, [(d["yT"][1024 + hd * 128: 1024 + (hd + 1) * 128, qs], yst[ys][:])], ["yst%d" % ys], (), "ast%d" % ys)

            head_loads(0)
            for i in range(LOOK):
                emit_S(i)
            for i, (hd, qi, kj, m, nk) in enumerate(its):
                hs = hd % 2
                hk = "hd%d" % hs
                if qi == 0 and kj == 3 and m == 0 and hd + 1 < 8:
                    head_loads(hd + 1)
                if i + LOOK < len(its):
                    emit_S(i + LOOK)
                b = sbank[i]
                e_ = P.ring("et", 6)
                c0 = c0_of(qi, kj)
                P.act(et[e_][:, c0:], sc[b][:, c0:], AF.Exp, ["sc%d" % b], ["et%d" % e_], scale=0.125)
                if kj >= 4 * qi:
                    base = qi * TW + c0 - kj * 128
                    P.op("gpsimd", lambda e, e_=e_, base=base, c0=c0: e.affine_select(
                        out=et[e_][:, c0:], in_=et[e_][:, c0:], pattern=[[1, TW - c0]], compare_op=ALU.is_ge, fill=0.0,
                        base=base, channel_multiplier=-1), ["et%d" % e_], ["et%d" % e_])
                P.mm([(O[m][:, c0:], vh[hs][:, kj, :], et[e_][:, c0:], kj == 0, kj == nk - 1)], [hk, "et%d" % e_], ["O%d" % m])
                P.mm([(Lp[m][:, c0:], self.onesb[:], et[e_][:, c0:], kj == 0, kj == nk - 1)], ["et%d" % e_], ["L%d" % m])
                n_emit = (i + 1) * N_SCAN_OPS // (len(its) * 3 // 4) - i * N_SCAN_OPS // (len(its) * 3 // 4)
                cur_unit[0] = i
                for _ in range(n_emit):
                    next(sgen, None)
                while deferred and deferred[0][0] <= i:
                    deferred.pop(0)[1]()
                if pend and i >= pend[0][0]:
                    epiB(*pend.pop(0)[1], sbank[i])
                if kj == nk - 1 and m == 1:
                    a_ = epiA(hd, qi)
                    pend.append((i + 5, (a_, hd, qi)))
            while pend:
                epiB(*pend.pop(0)[1], sbank[len(its) - 1])
            for _ in sgen:
                pass
            while deferred:
                deferred.pop(0)[1]()
            P.finalize()

    def phase_s5(self, l):
        self.phase_s5_half(l, 0)

    def phase_s5_half(self, l, half):
        nc, d = self.nc, self.d
        P = self.new_prog()
        NP = 32
        NO = NP // 4
        p0 = half * NP
        NK = len(KEXP)
        with ExitStack() as st:
            T = lambda n, s, dt: st.enter_context(nc.sbuf_tensor(self.tn(n), s, dt))
            sh3 = [128, NP, NK]
            pwr = T("s_pwr", sh3, F32); pwi = T("s_pwi", sh3, F32); tf = T("s_tf", sh3, F32)
            WBr = T("s_WBr", [128, NP, LCH, 32], BF16); WBi = T("s_WBi", [128, NP, LCH, 32], BF16)
            WCr = T("s_WCr", [128, NP, LCH, 32], BF16); WCi = T("s_WCi", [128, NP, LCH, 32], BF16)
            Cpr = T("s_Cpr", [128, NP, 32], BF16); Cpn = T("s_Cpn", [128, NP, 32], BF16)
            KTt = T("s_KT", [128, NO, LCH, 128], BF16)
            st2 = ExitStack()
            T2 = lambda n, s_, dt: st2.enter_context(nc.sbuf_tensor(self.tn(n), s_, dt))
            Tmain = T
            T = T2
            are = T("s_are", [128, NP], F32); aim = T("s_aim", [128, NP], F32); lst = T("s_lst", [128, NP], F32)
            Bre = T("s_Bre", [128, NP, 16], F32); Bim = T("s_Bim", [128, NP, 16], F32)
            Cre = T("s_Cre", [128, NP, 16], F32); Cim = T("s_Cim", [128, NP, 16], F32)
            P.dma("sync", [(are[:], d["are"][l, :, p0:p0 + NP]), (aim[:], d["aim"][l, :, p0:p0 + NP]), (lst[:], d["lst"][l, :, p0:p0 + NP]),
                           (Bre[:], d["Bre"][l, :, p0:p0 + NP, :]), (Bim[:], d["Bim"][l, :, p0:p0 + NP, :]),
                           (Cre[:], d["Cre"][l, :, p0:p0 + NP, :]), (Cim[:], d["Cim"][l, :, p0:p0 + NP, :])], (), ["prm"], "prm")
            zr = T("s_zr", [128, NP], F32); zi = T("s_zi", [128, NP], F32)
            P.act(lst[:], lst[:], AF.Exp, ["prm"], ["step"])
            P.tt("vector", zr[:], are[:], lst[:], ALU.mult, ["prm", "step"], ["zr"])
            P.tt("vector", zi[:], aim[:], lst[:], ALU.mult, ["prm", "step"], ["zi"])
            kz = T("s_kz", sh3, F32); mag = T("s_mag", sh3, F32); argt = T("s_arg", sh3, F32)
            ti = T("s_ti", sh3, I32)
            kb = self.kvec[:].unsqueeze(1).broadcast_to(sh3)
            P.tt("vector", kz[:], zr[:].unsqueeze(2).broadcast_to(sh3), kb, ALU.mult, ["zr"], ["kz"])
            P.act(mag[:], kz[:], AF.Exp, ["kz"], ["mag"])
            P.tt("vector", kz[:], zi[:].unsqueeze(2).broadcast_to(sh3), kb, ALU.mult, ["zi", "mag"], ["kz"])
            self.rr(P, argt[:], kz[:], tf[:], ti[:], "kz", "arg", "rt")
            P.act(pwi[:], argt[:], AF.Sin, ["arg"], ["pwi"])
            P.tt("vector", pwi[:], pwi[:], mag[:], ALU.mult, ["pwi", "mag"], ["pwi"])
            P.ts("vector", kz[:], kz[:], math.pi / 2, None, ALU.add, None, ["kz", "pwi"], ["kz"])
            self.rr(P, argt[:], kz[:], tf[:], ti[:], "kz", "arg", "rt")
            P.act(pwr[:], argt[:], AF.Sin, ["arg"], ["pwr"])
            P.tt("vector", pwr[:], pwr[:], mag[:], ALU.mult, ["pwr", "mag"], ["pwr"])
            nr = T("s_nr", [128, NP], F32); den = T("s_den", [128, NP], F32); t0 = T("s_t0", [128, NP], F32)
            kr = T("s_kr", [128, NP], F32); ki = T("s_ki", [128, NP], F32)
            P.ts("vector", nr[:], pwr[:, :, 1], -1.0, None, ALU.add, None, ["pwr"], ["nr"])
            P.tt("vector", den[:], are[:], are[:], ALU.mult, ["prm"], ["den"])
            P.tt("vector", t0[:], aim[:], aim[:], ALU.mult, ["prm"], ["t0"])
            P.tt("vector", den[:], den[:], t0[:], ALU.add, ["den", "t0"], ["den"])
            P.op("vector", lambda e: e.reciprocal(out=den[:], in_=den[:]), ["den"], ["den"])
            P.tt("vector", kr[:], nr[:], are[:], ALU.mult, ["nr", "prm"], ["kr"])
            P.tt("vector", t0[:], pwi[:, :, 1], aim[:], ALU.mult, ["pwi", "prm", "den"], ["t0"])
            P.tt("vector", kr[:], kr[:], t0[:], ALU.add, ["kr", "t0"], ["kr"])
            P.tt("vector", kr[:], kr[:], den[:], ALU.mult, ["kr", "den"], ["kr"])
            P.tt("vector", ki[:], pwi[:, :, 1], are[:], ALU.mult, ["pwi", "prm"], ["ki"])
            P.tt("vector", t0[:], nr[:], aim[:], ALU.mult, ["nr", "prm", "kr"], ["t0"])
            P.tt("vector", ki[:], ki[:], t0[:], ALU.subtract, ["ki", "t0"], ["ki"])
            P.tt("vector", ki[:], ki[:], den[:], ALU.mult, ["ki", "den"], ["ki"])
            sh16 = [128, NP, 16]
            Bbr = T("s_Bbr", sh16, F32); Bbi = T("s_Bbi", sh16, F32); t16 = T("s_t16", sh16, F32)
            krb = kr[:].unsqueeze(2).broadcast_to(sh16); kib = ki[:].unsqueeze(2).broadcast_to(sh16)
            P.tt("vector", Bbr[:], Bre[:], krb, ALU.mult, ["prm", "kr"], ["Bbr"])
            P.tt("vector", t16[:], Bim[:], kib, ALU.mult, ["prm", "ki"], ["t16"])
            P.tt("vector", Bbr[:], Bbr[:], t16[:], ALU.subtract, ["Bbr", "t16"], ["Bbr"])
            P.tt("vector", Bbi[:], Bim[:], krb, ALU.mult, ["prm", "kr"], ["Bbi"])
            P.tt("vector", t16[:], Bre[:], kib, ALU.mult, ["prm", "ki", "Bbr"], ["t16"])
            P.tt("vector", Bbi[:], Bbi[:], t16[:], ALU.add, ["Bbi", "t16"], ["Bbi"])
            for i_, tbl in enumerate((WBr, WBi, WCr, WCi, Cpr, Cpn)):
                P.memset("gpsimd", tbl[:], 0.0, ["tbl%d" % i_])
            P.memset("gpsimd", KTt[:], 0.0, ["KT"])
            sh4 = [128, NP, LCH, 16]
            ta = T("s_ta", sh4, F32)
            tb = T("s_tb", sh4, F32)

            def pw(tbl, k0):
                return tbl[:, :, k0:k0 + LCH].unsqueeze(3).broadcast_to(sh4)

            def bc(tbl):
                return tbl[:, :, :].unsqueeze(2).broadcast_to(sh4)

            def place(dst_tbl, tblidx, name, neg=False):
                for g in range(2):
                    rs = slice(g * 64, (g + 1) * 64)
                    colsl = slice(g * 16, (g + 1) * 16)
                    if neg:
                        P.act(dst_tbl[rs, :, :, colsl], ta[rs], AF.Copy, ["ta", "tbl%d" % tblidx], ["%s%d" % (name, g)], scale=-1.0)
                    else:
                        P.copy("scalar" if g == 0 else "vector", dst_tbl[rs, :, :, colsl], ta[rs], ["ta", "tbl%d" % tblidx], ["%s%d" % (name, g)])

            P.tt("vector", ta[:], pw(pwr, 0), bc(Bbr), ALU.mult, ["pwr", "Bbr"], ["ta"])
            P.tt("gpsimd", tb[:], pw(pwi, 0), bc(Bbi), ALU.mult, ["pwi", "Bbi"], ["tb"])
            P.tt("vector", ta[:], ta[:], tb[:], ALU.subtract, ["ta", "tb"], ["ta"])
            place(WBr, 0, "WBr")
            P.tt("vector", ta[:], pw(pwr, 0), bc(Bbi), ALU.mult, ["pwr", "Bbi", "WBr0", "WBr1"], ["ta"])
            P.tt("gpsimd", tb[:], pw(pwi, 0), bc(Bbr), ALU.mult, ["pwi", "Bbr", "ta"], ["tb"])
            P.tt("vector", ta[:], ta[:], tb[:], ALU.add, ["ta", "tb"], ["ta"])
            place(WBi, 1, "WBi")
            P.tt("vector", ta[:], pw(pwr, 1), bc(Cre), ALU.mult, ["pwr", "prm", "WBi0", "WBi1"], ["ta"])
            P.tt("gpsimd", tb[:], pw(pwi, 1), bc(Cim), ALU.mult, ["pwi", "prm", "ta"], ["tb"])
            P.tt("vector", ta[:], ta[:], tb[:], ALU.subtract, ["ta", "tb"], ["ta"])
            place(WCr, 2, "WCr")
            P.tt("vector", ta[:], pw(pwi, 1), bc(Cre), ALU.mult, ["pwi", "prm", "WCr0", "WCr1"], ["ta"])
            P.tt("gpsimd", tb[:], pw(pwr, 1), bc(Cim), ALU.mult, ["pwr", "prm", "ta"], ["tb"])
            P.tt("vector", ta[:], ta[:], tb[:], ALU.add, ["ta", "tb"], ["ta"])
            place(WCi, 3, "WCi", neg=True)
            for g in range(2):
                rs = slice(g * 64, (g + 1) * 64)
                colsl = slice(g * 16, (g + 1) * 16)
                P.copy("vector", Cpr[rs, :, colsl], Cre[rs], ["prm", "tbl4"], ["Cpr%d" % g])
                P.ts("vector", Cpn[rs, :, colsl], Cim[rs], -1.0, None, ALU.mult, None, ["prm", "tbl5"], ["Cpn%d" % g])
            P.ts("vector", tf[:], pwi[:], -1.0, None, ALU.mult, None, ["pwi", "pwr"], ["npwi"])
            T = Tmain
            WBk = ["WBr0", "WBr1", "WBi0", "WBi1"]
            WCk = ["WCr0", "WCr1", "WCi0", "WCi1"]
            pk = [self.PS(st, "s_pk%d" % i, [128, LCH, 32], F32) for i in range(2)]
            for pl in range(NP):
                o, q = pl // 4, pl % 4
                b = P.ring("pk", 2)
                rows = slice(32 * q, 32 * q + 32)
                items = []
                for lag in range(LCH):
                    tp = (0, 96) if q == 3 else None
                    items.append((pk[b][rows, lag, :], WBr[:, pl, lag, :], Cpr[:, pl, :], True, False, tp))
                    items.append((pk[b][rows, lag, :], WBi[:, pl, lag, :], Cpn[:, pl, :], False, True, tp))
                P.mm(items, WBk + ["Cpr0", "Cpr1", "Cpn0", "Cpn1"], ["pk%d" % b])
                P.copy("scalar", KTt[rows, o, :, 32 * q:32 * q + 32], pk[b][rows, :, :], ["pk%d" % b, "KT"], ["KTb%d" % pl])
            for o_ in range(NO):
                og_ = half * NO + o_
                P.stt(KTt[:, o_, 0, :], self.identf[:], self.ssmD[:, l, og_:og_ + 1], KTt[:, o_, 0, :], ALU.mult, ALU.add,
                      ["KTb%d" % (o_ * 4 + q_) for q_ in range(4)], ["KTd%d" % o_])
            KTk = ["KTb%d" % pl for pl in range(NP)] + ["KTd%d" % o_ for o_ in range(NO)]
            st2.close()
            P.fence()
            ur = T("s_ur", [128, S], BF16)
            uo = [T("s_uo%d" % i, [128, LCH, NCH], BF16) for i in range(2)]
            WBl = [T("s_WBl%d" % i, [128, 2, LCH, 128], BF16) for i in range(4)]
            sst = [T("s_sst%d" % i, [128, 2, NCH], F32) for i in range(3)]
            for q_ in range(4):
                P.memset("gpsimd", WBl[q_][:], 0.0, ["WBl%d_0" % q_, "WBl%d_1" % q_])
            pT = [self.PS(st, "s_pT%d" % i, [128, LCH, 128], BF16) for i in range(1)]
            pS = [self.PS(st, "s_pS%d" % i, [128, 2, NCH], F32) for i in range(2)]
            P.dma(STENG, [(d["WCs"][0:128, :], WCr[:].rearrange("p a b c -> p (a b c)")),
                          (d["WCs"][128:256, :], WCi[:].rearrange("p a b c -> p (a b c)")),
                          (d["KTs"], KTt[:].rearrange("p a b c -> p (a b c)")),
                          (d["PWs"][0:128, :], pwr[:].rearrange("p a b -> p (a b)")),
                          (d["PWs"][128:256, :], pwi[:].rearrange("p a b -> p (a b)")),
                          (d["PWs"][256:384, :], tf[:].rearrange("p a b -> p (a b)"))],
                  WCk + KTk + ["pwr", "pwi", "npwi"], (), "tblst")
            agen = iter(())
            if l + 1 < self.depth:
                wad = [T("s_wad%d" % i, [128, 16, 128], F32) for i in range(3)]
                psm = self.PS(st, "s_psm", [128, 48], F32)
                agen = self.adaln_gen(P, l + 1, wad, psm)
            for o in range(NO):
                og = o
                us = o % 2
                P.dma("sync", [(ur[:], d["uT"][og * 128:(og + 1) * 128, :])], (), ["ur"], "ur")
                P.copy("scalar", uo[us][:], ur[:].rearrange("p (c j) -> p j c", j=LCH), ["ur"], ["uo%d" % us])
                P.dma(STENG, [(d["uDe"][og * 128:(og + 1) * 128, :], uo[us][:].rearrange("p j c -> p (j c)"))], ["uo%d" % us], (), "udst%d" % us)
                for q in range(4):
                    pl = o * 4 + q
                    rows = slice(32 * q, 32 * q + 32)
                    tp = (0, 96) if q == 3 else None
                    for ri, tbl in enumerate((WBr, WBi)):
                        P.tr([(pT[0][rows, kk, :], tbl[:, pl, kk, :], tp) for kk in range(LCH)], self.identb[:], WBk, ["pT0"])
                        P.copy("scalar" if ri == 0 else "vector", WBl[q][rows, ri, :, :], pT[0][rows, :, :], ["pT0"], ["WBl%d_%d" % (q, ri)])
                    sb = P.ring("pS", 2)
                    items = []
                    for ri in range(2):
                        for j in range(LCH):
                            items.append((pS[sb][:, ri, :], WBl[q][:, ri, LCH - 1 - j, :], uo[us][:, j, :], j == 0, j == LCH - 1))
                    P.mm(items, ["WBl%d_0" % q, "WBl%d_1" % q, "uo%d" % us], ["pS%d" % sb])
                    r_ = P.ring("sst", 3)
                    P.copy("vector" if pl % 2 == 0 else "scalar", sst[r_][:], pS[sb][:], ["pS%d" % sb], ["sst%d" % r_])
                    P.dma(STENG, [(d["Sloc"][pl * 128:(pl + 1) * 128, :], sst[r_][:].rearrange("p r c -> p (r c)"))], ["sst%d" % r_], (), "sstd%d" % r_)
                    next(agen, None)
                    next(agen, None)
            for _ in agen:
                pass
            P.finalize()

    def phase_s5b(self, l):
        nc, d = self.nc, self.d
        P = self.new_prog()
        with ExitStack() as st:
            T = lambda n, s, dt: st.enter_context(nc.sbuf_tensor(self.tn(n), s, dt))
            WCr = T("b_WCr", [128, 32, LCH, 32], BF16); WCi = T("b_WCi", [128, 32, LCH, 32], BF16)
            KTt = T("b_KT", [128, 8, LCH, 128], BF16)
            uo = [T("b_uo%d" % i, [128, LCH, NCH], BF16) for i in range(2)]
            Xp = [[T("b_Xp%d_%d" % (ob, q), [128, 2, NCH], BF16) for q in range(4)] for ob in range(2)]
            yg = [T("b_yg%d" % i, [128, S], BF16) for i in range(2)]
            pY = [self.PS(st, "b_pY%d" % i, [128, NCH], F32) for i in range(4)]
            P.dma("sync", [(WCr[:].rearrange("p a b c -> p (a b c)"), d["WCs"][0:128, :]),
                           (WCi[:].rearrange("p a b c -> p (a b c)"), d["WCs"][128:256, :]),
                           (KTt[:].rearrange("p a b c -> p (a b c)"), d["KTs"])], (), ["tbl"], "tbl")

            def loads(o):
                us = o % 2
                pairs = [(uo[us][:].rearrange("p j c -> p (j c)"), d["uDe"][o * 128:(o + 1) * 128, :])]
                for q in range(4):
                    pl = o * 4 + q
                    pairs.append((Xp[us][q][:].rearrange("p r c -> p (r c)"), d["Xps"][pl * 128:(pl + 1) * 128, :]))
                P.dma("sync", pairs, (), ["in%d" % us], "in%d" % us)

            loads(0)
            for o in range(8):
                us = o % 2
                if o + 1 < 8:
                    loads(o + 1)
                uov = uo[us][:].rearrange("p j c -> p c j")
                ygv = yg[us][:].rearrange("p (c j) -> p c j", j=LCH)
                for i in range(LCH):
                    b = P.ring("pY", 4)
                    items = []
                    for j in range(i + 1):
                        items.append((pY[b][:], KTt[:, o, i - j, :], uov[:, :, j], j == 0, False))
                    for q in range(4):
                        pl = o * 4 + q
                        rows = slice(32 * q, 32 * q + 32)
                        tp = (0, 96) if q == 3 else None
                        items.append((pY[b][rows, :], WCr[:, pl, i, :], Xp[us][q][:, 0, :], False, False, tp))
                        items.append((pY[b][rows, :], WCi[:, pl, i, :], Xp[us][q][:, 1, :], False, True, tp))
                    P.mm(items, ["tbl", "in%d" % us], ["pY%d" % b])
                    P.act(ygv[:, :, i], pY[b][:], AF.Gelu_apprx_tanh, ["pY%d" % b], ["yg%d_%d" % (us, i)])
                P.dma(STENG, [(d["ygT"][o * 128:(o + 1) * 128, :], yg[us][:])], ["yg%d_%d" % (us, i) for i in range(LCH)], (), "ygst%d" % us)
            P.finalize()

    def phase_glu(self, l):
        nc, d = self.nc, self.d
        P = self.new_prog()
        Wd = d["w_glu"][l]
        KT = 8
        with ExitStack() as st:
            T = lambda n, s, dt: st.enter_context(nc.sbuf_tensor(self.tn(n), s, dt))
            wst = [T("gwst%d" % i, [128, KT, 256], F32) for i in range(2)]
            wbuf = [T("gwbuf%d" % i, [128, KT, 1024], BF16) for i in range(2)]
            actt = [T("gact%d" % i, [128, KT, TW], BF16) for i in range(2)]
            zst = [T("gzs%d" % i, [128, 4, TW], BF16) for i in range(2)]
            stg = [T("gstg%d" % i, [128, 4, TW], BF16) for i in range(2)]
            sg_ = [T("gsig%d" % i, [128, TW], F32) for i in range(3)]
            tg_ = [T("gt%d" % i, [128, TW], F32) for i in range(3)]
            ps = [self.PS(st, "gps%d" % i, [128, TW], F32) for i in range(8)]
            av = d["ygT"].rearrange("(k p) t -> p k t", p=128)
            gchunks = [[(512 * gi, "nat"), (512 * gi + 256, "nat"), (1024 + 512 * gi, "nat"), (1024 + 512 * gi + 256, "nat")] for gi in range(2)]
            iters = [(gi, n) for gi in range(2) for n in range(NT)]
            aslot, zslot = {}, {}

            def emit_loads(idx):
                gi_, n_ = iters[idx]
                a_ = P.ring("actt", 2)
                z_ = P.ring("zs", 2)
                aslot[idx], zslot[idx] = a_, z_
                cs_ = slice(n_ * TW, (n_ + 1) * TW)
                P.dma("sync", [(actt[a_][:], av[:, :, cs_])], (), ["act%d" % a_], "act%d" % a_)
                P.dma("sync", [(zst[z_][:], d["zsT"].rearrange("(j p) t -> p j t", p=128)[:, gi_ * 4:(gi_ + 1) * 4, cs_])], (), ["zs%d" % z_], "zs%d" % z_)
                if gi_ + 1 < 2 and 1 <= n_ <= 4:
                    self.load_wchunk(P, Wd, KT, gchunks[gi_ + 1], n_ - 1, wst, wbuf, gi_ + 1, "w%d" % (gi_ + 1))

            self.load_wgroup(P, Wd, KT, gchunks[0], wst, wbuf, 0, "w0")
            emit_loads(0)
            for idx, (gi, n) in enumerate(iters):
                slot = gi
                gkey = "w%d" % slot
                wk = self.wkeys(gkey, gchunks[gi])
                wb = wbuf[slot]
                if idx + 1 < len(iters):
                    emit_loads(idx + 1)
                a, z = aslot[idx], zslot[idx]
                cs = slice(n * TW, (n + 1) * TW)
                sg = P.ring("stg", 2)
                for j in range(4):
                    bA = P.ring("ps", 8)
                    P.mm([(ps[bA][:], wb[:, k, j * 128:(j + 1) * 128], actt[a][:, k, :], k == 0, k == KT - 1) for k in range(KT)],
                         ["act%d" % a] + wk, ["ps%d" % bA])
                    bB = P.ring("ps", 8)
                    P.mm([(ps[bB][:], wb[:, k, 512 + j * 128:512 + (j + 1) * 128], actt[a][:, k, :], k == 0, k == KT - 1) for k in range(KT)],
                         ["act%d" % a] + wk, ["ps%d" % bB])
                    mt = gi * 4 + j
                    r = P.ring("sig", 3)
                    P.act(sg_[r][:], ps[bB][:], AF.Sigmoid, ["ps%d" % bB], ["sig%d" % r], bias=self.bglu[:, l, 8 + mt: 9 + mt])
                    P.stt(tg_[r][:], ps[bA][:], self.bglu[:, l, mt:mt + 1], sg_[r][:], ALU.add, ALU.mult, ["ps%d" % bA, "sig%d" % r], ["tg%d" % r])
                    P.tt("gpsimd", stg[sg][:, j, :], tg_[r][:], zst[z][:, j, :], ALU.mult, ["tg%d" % r, "zs%d" % z], ["stg%d_%d" % (sg, j)])
                dv = d["yT"].rearrange("(j p) t -> p j t", p=128)[:, gi * 4:(gi + 1) * 4, cs]
                P.dma(STENG, [(dv, stg[sg][:])], ["stg%d_%d" % (sg, j) for j in range(4)], (), "sst%d" % sg)
            P.finalize()

    def phase_outproj(self, l, xsrc):
        nc, d = self.nc, self.d
        P = self.new_prog()
        Wd = d["w_out"][l]
        KT = 16
        with ExitStack() as st:
            T = lambda n, s, dt: st.enter_context(nc.sbuf_tensor(self.tn(n), s, dt))
            wst = [T("owst%d" % i, [128, KT, 256], F32) for i in range(2)]
            wbuf = [T("owbuf%d" % i, [128, KT, 1024], BF16) for i in range(2)]
            actt = [T("oact%d" % i, [128, KT, TW], BF16) for i in range(2)]
            xo = [T("oxo%d" % i, [128, 8, TW], F32) for i in range(2)]
            ps = [self.PS(st, "ops%d" % i, [128, TW], F32) for i in range(8)]
            av = d["yT"].rearrange("(k p) t -> p k t", p=128)
            xv = xsrc.rearrange("(j p) t -> p j t", p=128)
            ov = d["xres"].rearrange("(j p) t -> p j t", p=128)
            gchunks = [[(1024 * gi + 256 * ci, "nat") for ci in range(4)] for gi in range(2)]
            iters = [(gi, n) for gi in range(2) for n in range(NT)]
            aslot, xslot = {}, {}

            def emit_loads(idx):
                gi_, n_ = iters[idx]
                a_ = P.ring("actt", 2)
                x__ = P.ring("xo", 2)
                aslot[idx], xslot[idx] = a_, x__
                cs_ = slice(n_ * TW, (n_ + 1) * TW)
                P.dma("sync", [(actt[a_][:], av[:, :, cs_])], (), ["act%d" % a_], "act%d" % a_)
                P.dma("sync", [(xo[x__][:], xv[:, gi_ * 8:(gi_ + 1) * 8, cs_])], (), ["xo%d" % x__], "xo%d" % x__)
                if gi_ + 1 < 2 and 1 <= n_ <= 4:
                    self.load_wchunk(P, Wd, KT, gchunks[gi_ + 1], n_ - 1, wst, wbuf, gi_ + 1, "w%d" % (gi_ + 1))

            self.load_wgroup(P, Wd, KT, gchunks[0], wst, wbuf, 0, "w0")
            emit_loads(0)
            for idx, (gi, n) in enumerate(iters):
                slot = gi
                gkey = "w%d" % slot
                wk = self.wkeys(gkey, gchunks[gi])
                wb = wbuf[slot]
                if idx + 1 < len(iters):
                    emit_loads(idx + 1)
                a, x_ = aslot[idx], xslot[idx]
                cs = slice(n * TW, (n + 1) * TW)
                for j in range(8):
                    b = P.ring("ps", 8)
                    P.mm([(ps[b][:], wb[:, k, j * 128:(j + 1) * 128], actt[a][:, k, :], k == 0, k == KT - 1) for k in range(KT)],
                         ["act%d" % a] + wk, ["ps%d" % b])
                    mt = gi * 8 + j
                    P.stt(xo[x_][:, j, :], ps[b][:], self.mod[:, l, 32 + mt:33 + mt], xo[x_][:, j, :], ALU.mult, ALU.add,
                          ["ps%d" % b, "xo%d" % x_], ["xn%d_%d" % (x_, j)])
                P.dma(STENG, [(ov[:, gi * 8:(gi + 1) * 8, cs], xo[x_][:])], ["xn%d_%d" % (x_, j) for j in range(8)], ["xo%d" % x_], "xst%d" % x_)
            P.finalize()


def _prep_shared(inp):
    L = DEPTH
    f = np.float32
    sh = {}
    sh["invf"] = np.tile((10000.0 ** (-np.arange(32, dtype=np.float64) / 32)).astype(f), 4).reshape(128, 1)
    sh["kvec"] = np.tile(np.asarray(KEXP, dtype=f)[None, :], (128, 1))
    sh["normg"] = np.ascontiguousarray(inp["norm_g"].reshape(L, 16, 128).transpose(2, 0, 1))
    sh["w_ada"] = inp["w_ada"]
    sh["bada"] = np.ascontiguousarray(inp["b_ada"].reshape(L, 48, 128).transpose(2, 0, 1))
    sh["w_in"] = inp["w_in"]
    sh["w_out"] = inp["w_out"]
    sh["w_glu"] = inp["w_glu"]
    sh["bglu"] = np.ascontiguousarray(inp["b_glu"].reshape(L, 16, 128).transpose(2, 0, 1))

    def st_layout(a):
        return np.ascontiguousarray(a.reshape(L, 32, 2, 64).transpose(0, 2, 3, 1).reshape(L, 128, 32))
    sh["are"] = st_layout(inp["ssm_a_re"])
    sh["aim"] = st_layout(inp["ssm_a_im"])
    sh["lst"] = st_layout(np.broadcast_to(inp["ssm_log_step"][:, :, None], (L, 64, 64)))

    def bc_layout(a):
        return np.ascontiguousarray(a.reshape(L, 32, 2, 64, 16).transpose(0, 2, 3, 1, 4).reshape(L, 128, 32, 16))
    sh["Bre"] = bc_layout(inp["ssm_b_re"])
    sh["Bim"] = bc_layout(inp["ssm_b_im"])
    sh["Cre"] = bc_layout(inp["ssm_c_re"].transpose(0, 1, 3, 2))
    sh["Cim"] = bc_layout(inp["ssm_c_im"].transpose(0, 1, 3, 2))
    sh["ssmD"] = np.ascontiguousarray(inp["ssm_d"].reshape(L, 8, 128).transpose(2, 0, 1))
    for a, b in (("lq1", "lam_q1"), ("lk1", "lam_k1"), ("lq2", "lam_q2"), ("lk2", "lam_k2")):
        sh[a] = np.ascontiguousarray(np.broadcast_to(inp[b][None, :, :], (128, L, 64)))
    sh["subg"] = np.ascontiguousarray(inp["sub_g"].T)
    sh["fing"] = np.ascontiguousarray(inp["final_g"].reshape(16, 128).T)
    return {k: np.ascontiguousarray(v) for k, v in sh.items()}


def _run(inp, depth=DEPTH, dbg=None, trace=False, stop=99, ncores=None):
    inp = {k: np.asarray(v) for k, v in inp.items()}
    B = ncores or inp["x"].shape[0]
    bld = Builder(depth=depth, dbg=dbg, stop=stop)
    nc = bld.build()
    sh = _prep_shared(inp)
    in_maps = []
    for b in range(B):
        m = dict(sh)
        m["xT"] = np.ascontiguousarray(inp["x"][b].T)
        m["c_l"] = np.ascontiguousarray(inp["c"][b].reshape(16, 128).T)
        m["pos"] = np.ascontiguousarray(inp["positions"][b].reshape(1, S).astype(np.int32))
        in_maps.append(m)
    res = run_bass_kernel_spmd(nc, in_maps, core_ids=list(range(B)), trace=trace)
    return res


def kernel(**inputs):
    res = _run(inputs)
    out = np.stack([np.ascontiguousarray(r["outT"].T) for r in res.results], axis=0)
    return out.astype(np.float32)
```
